# Optimizing a Trainium2 kernel written in Bass

```python
import jax, jax.numpy as jnp
from jax import lax
import numpy as np

D_MODEL = 1024
BATCH = 8
SEQ = 4096
DEPTH = 2

N_MIXERS = 2
N_RET = (DEPTH + 1) // 2
N_FOX = DEPTH // 2
D_PLE = 256
D_FF = 2816
EPS = 1e-6
RET_DK = 256
RET_HEADS = D_MODEL // RET_DK
RET_DV = 2 * RET_DK
RET_CHUNK = 128
ROPE_BASE = 10000.0
RET_IN = RET_HEADS * (2 * RET_DK + 2 * RET_DV)
FOX_DH = 64
FOX_HEADS = D_MODEL // FOX_DH
FOX_BLOCK = 128
FOX_IN = 3 * FOX_HEADS * FOX_DH + FOX_HEADS

kernel_name = "hybrid_retention_forgetting_attention_macaron"


def rmsnorm(x, w):
    xf = x.astype(jnp.float32)
    y = xf * lax.rsqrt(jnp.mean(xf * xf, axis=-1, keepdims=True) + EPS)
    return (y * w.astype(jnp.float32)).astype(x.dtype)


def swiglu(x, w_in, w_out):
    g, u = jnp.split(x @ w_in, 2, axis=-1)
    return (jax.nn.silu(g) * u) @ w_out


def rotary(x, pos):
    half = x.shape[-1] // 2
    inv_freq = ROPE_BASE ** (-jnp.arange(half, dtype=jnp.float32) / half)
    ang = pos[:, None] * inv_freq[None, :]
    cos = jnp.cos(ang)[None, :, None, :]
    sin = jnp.sin(ang)[None, :, None, :]
    xf = x.astype(jnp.float32)
    x1, x2 = xf[..., :half], xf[..., half:]
    return jnp.concatenate([x1 * cos - x2 * sin, x1 * sin + x2 * cos], axis=-1).astype(x.dtype)


def retention(h, w_in, gn_w, w_out):
    B, S, _ = h.shape
    H, DK, DV, C = RET_HEADS, RET_DK, RET_DV, RET_CHUNK
    NC = S // C
    q, k, v, g = jnp.split(h @ w_in, [H * DK, 2 * H * DK, 2 * H * DK + H * DV], axis=-1)
    pos = jnp.arange(S, dtype=jnp.float32)
    q = rotary(q.reshape(B, S, H, DK), pos)
    k = rotary(k.reshape(B, S, H, DK), pos) * (DK ** -0.5)
    v = v.reshape(B, S, H, DV)
    log_gamma = jnp.log1p(-jnp.exp2(-5.0 - jnp.arange(H, dtype=jnp.float32)))
    idx = jnp.arange(C, dtype=jnp.float32)
    diff = idx[:, None] - idx[None, :]
    decay_intra = jnp.where(diff[None] >= 0,
                            jnp.exp(log_gamma[:, None, None] * jnp.maximum(diff, 0.0)[None]), 0.0)
    zeta = jnp.exp(log_gamma[:, None] * (C - 1 - idx)[None, :])
    xi = jnp.exp(log_gamma[:, None] * (idx + 1)[None, :]).T
    gamma_c = jnp.exp(log_gamma * C)

    def to_chunks(t):
        return t.reshape(B, NC, C, H, t.shape[-1]).transpose(1, 0, 2, 3, 4)

    def step(state, inp):
        qc, kc, vc = inp
        qf, kf, vf = (t.astype(jnp.float32) for t in (qc, kc, vc))
        s = jnp.einsum('bqhd,bkhd->bhqk', qf, kf) * decay_intra[None]
        inner = jnp.einsum('bhqk,bkhe->bqhe', s, vf)
        cross = jnp.einsum('bqhd,bhde->bqhe', qf, state) * xi[None, :, :, None]
        new_state = gamma_c[None, :, None, None] * state + jnp.einsum('bkhd,hk,bkhe->bhde', kf, zeta, vf)
        return new_state, (inner + cross).astype(qc.dtype)

    state0 = jnp.zeros((B, H, DK, DV), jnp.float32)
    _, o = lax.scan(step, state0, (to_chunks(q), to_chunks(k), to_chunks(v)))
    o = o.transpose(1, 0, 2, 3, 4).reshape(B, S, H, DV)
    o = rmsnorm(o, gn_w)
    y = jax.nn.silu(g) * o.reshape(B, S, H * DV)
    return y @ w_out


def forgetting_attention(h, w_in, b_f, w_out):
    B, S, D = h.shape
    H, DH, Q = FOX_HEADS, FOX_DH, FOX_BLOCK
    NB = S // Q
    q, k, v, fz = jnp.split(h @ w_in, [H * DH, 2 * H * DH, 3 * H * DH], axis=-1)
    q = q.reshape(B, S, H, DH).transpose(0, 2, 1, 3)
    k = k.reshape(B, S, H, DH).transpose(0, 2, 1, 3)
    v = v.reshape(B, S, H, DH).transpose(0, 2, 1, 3)
    log_f = jax.nn.log_sigmoid(fz.astype(jnp.float32) + b_f.astype(jnp.float32))
    c = jnp.cumsum(log_f, axis=1).transpose(0, 2, 1)
    kpos = jnp.arange(S)
    scale = DH ** -0.5

    def block(i):
        start = i * Q
        qb = lax.dynamic_slice_in_dim(q, start, Q, axis=2)
        cb = lax.dynamic_slice_in_dim(c, start, Q, axis=2)
        logits = (jnp.einsum('bhqd,bhkd->bhqk', qb, k).astype(jnp.float32) * scale
                  + cb[..., None] - c[:, :, None, :])
        qpos = start + jnp.arange(Q)
        logits = jnp.where((kpos[None, :] <= qpos[:, None])[None, None], logits, -jnp.inf)
        w = jax.nn.softmax(logits, axis=-1)
        return jnp.einsum('bhqk,bhkd->bhqd', w.astype(v.dtype), v)

    o = lax.map(block, jnp.arange(NB))
    o = o.transpose(1, 0, 3, 2, 4).reshape(B, S, H * DH)
    return o @ w_out


def setup_inputs(seed: int = 0) -> dict:
    key = jax.random.key(seed)
    ks = jax.random.split(key, 14)
    f32 = jnp.float32
    nrm = lambda k, shape, fan: jax.random.normal(k, shape, f32) * (fan ** -0.5)
    return {
        "x": jax.random.normal(ks[0], (BATCH, SEQ, D_MODEL), f32),
        "p": jax.random.normal(ks[1], (DEPTH, BATCH, SEQ, D_PLE), f32),
        "norm_w": 1.0 + 0.02 * jax.random.normal(ks[2], (DEPTH, 4, D_MODEL), f32),
        "ffn_w_in": nrm(ks[3], (DEPTH, 2, D_MODEL, 2 * D_FF), D_MODEL),
        "ffn_w_out": nrm(ks[4], (DEPTH, 2, D_FF, D_MODEL), D_FF),
        "ret_w_in": nrm(ks[5], (N_RET, D_MODEL, RET_IN), D_MODEL),
        "ret_gn_w": 1.0 + 0.02 * jax.random.normal(ks[6], (N_RET, RET_HEADS, RET_DV), f32),
        "ret_w_out": nrm(ks[7], (N_RET, RET_HEADS * RET_DV, D_MODEL), RET_HEADS * RET_DV),
        "fox_w_in": nrm(ks[8], (N_FOX, D_MODEL, FOX_IN), D_MODEL),
        "fox_b_f": jax.random.uniform(ks[9], (N_FOX, FOX_HEADS), f32, 1.0, 4.0),
        "fox_w_out": nrm(ks[10], (N_FOX, D_MODEL, D_MODEL), D_MODEL),
        "ple_w_proj": nrm(ks[11], (DEPTH, D_PLE, D_MODEL), D_PLE),
        "ple_w_gate": nrm(ks[12], (DEPTH, D_MODEL, D_MODEL), D_MODEL),
        "final_norm_w": 1.0 + 0.02 * jax.random.normal(ks[13], (D_MODEL,), f32),
    }


def reference(x, p, norm_w, ffn_w_in, ffn_w_out, ret_w_in, ret_gn_w, ret_w_out,
              fox_w_in, fox_b_f, fox_w_out, ple_w_proj, ple_w_gate, final_norm_w):
    h = x
    for i in range(DEPTH):
        nw = norm_w[i]
        h = h + 0.5 * swiglu(rmsnorm(h, nw[0]), ffn_w_in[i, 0], ffn_w_out[i, 0])
        hn = rmsnorm(h, nw[1])
        j = i // N_MIXERS
        if i % N_MIXERS == 0:
            h = h + retention(hn, ret_w_in[j], ret_gn_w[j], ret_w_out[j])
        else:
            h = h + forgetting_attention(hn, fox_w_in[j], fox_b_f[j], fox_w_out[j])
        h = h + 0.5 * swiglu(rmsnorm(h, nw[2]), ffn_w_in[i, 1], ffn_w_out[i, 1])
        gate = jax.nn.sigmoid(rmsnorm(h, nw[3]) @ ple_w_gate[i])
        h = h + gate * (p[i] @ ple_w_proj[i])
    return rmsnorm(h, final_norm_w)
```

```python
import numpy as np
from contextlib import ExitStack
import concourse.bass as bass
import concourse.mybir as mybir
from concourse.bass_utils import run_bass_kernel_spmd

F32 = mybir.dt.float32
BF16 = mybir.dt.bfloat16
AF = mybir.ActivationFunctionType
ALU = mybir.AluOpType

D = 1024
DFF = 2816
NFC = DFF // 128
DPLE = 256
DEPTH = 2
EPS = 1e-6
RET_H, RET_DK, RET_DV, RET_C = 4, 256, 512, 128
RET_IN = 6144
FOX_H, FOX_DH = 16, 64
FOX_IN = 3088
NEG = -30000.0
USE_POW = True

ENGS = ["pe", "act", "dve", "pool", "sp"]


class Buf:
    __slots__ = ("w", "rs", "sem")

    def __init__(self):
        self.w = None
        self.rs = []
        self.sem = {}


class DSem:
    __slots__ = ("sem", "count", "last")

    def __init__(self, sem):
        self.sem = sem
        self.count = 0
        self.last = None


class Op:
    __slots__ = ("eng", "fn", "deps", "sig", "cnt", "dsem", "dcnt", "epoch")

    def __init__(self, eng, fn, epoch):
        self.eng = eng
        self.fn = fn
        self.deps = []
        self.sig = False
        self.cnt = 0
        self.dsem = None
        self.dcnt = 0
        self.epoch = epoch


class Sched:
    def __init__(self, nc):
        self.nc = nc
        self.ops = {e: [] for e in ENGS}
        self.esem = {e: nc.alloc_semaphore(name="es_" + e) for e in ENGS}
        self.nsem = 0
        self.free = {e: [] for e in ENGS}
        self.live = []
        self.epoch = 0

    def _track(self, op, r, w):
        deps = []
        for b in r:
            if b.w is not None:
                deps.append(b.w)
        for b in w:
            if b.w is not None:
                deps.append(b.w)
            deps.extend(b.rs)
        for b in w:
            b.w = op
            b.rs = []
        for b in r:
            b.rs.append(op)
        seen = set()
        for d in deps:
            if d is op or id(d) in seen or d.epoch < self.epoch:
                continue
            if d.eng == "pe" and op.eng == "pe":
                continue
            seen.add(id(d))
            op.deps.append(d)
            if d.dsem is None:
                d.sig = True

    def op(self, eng, fn, r=(), w=()):
        o = Op(eng, fn, self.epoch)
        self._track(o, r, w)
        self.ops[eng].append(o)
        return o

    def dma(self, eng, out, in_, r=(), w=(), chan=None):
        if eng not in chan.sem:
            if self.free[eng]:
                ds = self.free[eng].pop()
            else:
                ds = DSem(self.nc.alloc_semaphore(name="ds_%d" % self.nsem))
                self.nsem += 1
            chan.sem[eng] = ds
            self.live.append((chan, eng, ds))
        ds = chan.sem[eng]
        ds.count += 16

        def fn(e, out=out, in_=in_):
            return e.dma_start(out=out, in_=in_)
        o = Op(eng, fn, self.epoch)
        o.dsem = ds.sem
        o.dcnt = ds.count
        ds.last = o
        self._track(o, r, w)
        self.ops[eng].append(o)
        return o

    def barrier(self):
        lasts = []
        for e in ENGS:
            for o in reversed(self.ops[e]):
                if o.dsem is None and o.fn is not None:
                    o.sig = True
                    lasts.append(o)
                    break
        dlast = [ds.last for (_, _, ds) in self.live if ds.last is not None]
        for e in ENGS:
            o = Op(e, None, self.epoch)
            o.deps = list(lasts) + dlast
            self.ops[e].append(o)
        for (b, e, ds) in self.live:
            del b.sem[e]
            self.free[e].append(ds)
        self.live = []
        self.epoch += 1

    def emit(self):
        nc = self.nc
        for e in ENGS:
            c = 0
            for o in self.ops[e]:
                if o.dsem is None and o.sig:
                    c += 1
                    o.cnt = c

        def run(ename, eng):
            waited = {}
            for o in self.ops[ename]:
                need = {}
                for d in o.deps:
                    if d.dsem is not None:
                        s, v = d.dsem, d.dcnt
                    else:
                        s, v = self.esem[d.eng], d.cnt
                    if s.num not in need or need[s.num][1] < v:
                        need[s.num] = (s, v)
                for num, (s, v) in need.items():
                    if waited.get(num, 0) >= v:
                        continue
                    waited[num] = v
                    eng.wait_ge(s, v)
                if o.fn is None:
                    continue
                ins = o.fn(eng)
                if o.dsem is not None:
                    ins.then_inc(o.dsem, 16)
                elif o.sig:
                    ins.then_inc(self.esem[ename], 1)

        with nc.Block() as block:
            @block.tensor
            def _(e):
                run("pe", e)

            @block.scalar
            def _(e):
                run("act", e)

            @block.vector
            def _(e):
                run("dve", e)

            @block.gpsimd
            def _(e):
                run("pool", e)

            @block.sync
            def _(e):
                run("sp", e)


def build(T):
    NT = T // 512
    NS = T // 128
    nc = bass.Bass("TRN2", target_bir_lowering=False)

    def din(name, shape):
        return nc.dram_tensor(name, list(shape), F32, kind="ExternalInput").ap()

    x_in = din("x", [T, D])
    p_in = din("p", [DEPTH, T, DPLE])
    norm_w = din("norm_w", [DEPTH, 4, D])
    ffn_w_in = din("ffn_w_in", [DEPTH, 2, D, 2 * DFF])
    ffn_w_out = din("ffn_w_out", [DEPTH, 2, DFF, D])
    ret_w_in = din("ret_w_in", [D, RET_IN])
    ret_gn_w = din("ret_gn_w", [RET_H * RET_DV])
    ret_w_out = din("ret_w_out", [RET_H * RET_DV, D])
    fox_w_in = din("fox_w_in", [D, FOX_IN])
    fox_b_f = din("fox_b_f", [FOX_H, 1])
    fox_w_out = din("fox_w_out", [D, D])
    ple_w_proj = din("ple_w_proj", [DEPTH, DPLE, D])
    ple_w_gate = din("ple_w_gate", [DEPTH, D, D])
    final_norm_w = din("final_norm_w", [D])
    c_ident = din("c_ident", [128, 128])
    c_rope = din("c_rope", [2, 128, T])
    c_dmask = din("c_dmask", [128, 4, 128])
    c_xi = din("c_xi", [128, 8, 128])
    c_zeta = din("c_zeta", [128, 1024])
    c_maskneg = din("c_maskneg", [128, 128])
    out = nc.dram_tensor("out", [T, D], F32, kind="ExternalOutput").ap()

    H = nc.dram_tensor("H", [T, D], F32).ap()
    QKs = nc.dram_tensor("QKs", [16 * 128, T], BF16).ap()
    VGs = nc.dram_tensor("VGs", [T, 4096], BF16).ap()
    QA = nc.dram_tensor("QA", [FOX_H, 70, T], BF16).ap()
    KA = nc.dram_tensor("KA", [FOX_H, 70, T], BF16).ap()
    Vs = nc.dram_tensor("Vs", [FOX_H, 128, NS, 64], BF16).ap()

    S = Sched(nc)
    uid = [0]

    def un(name):
        uid[0] += 1
        return "%s_%d" % (name, uid[0])
    gam = [1.0 - 2.0 ** (-5.0 - h) for h in range(RET_H)]
    gamma_c = [float(np.exp(np.log1p(-2.0 ** (-5.0 - h)) * RET_C)) for h in range(RET_H)]

    with ExitStack() as g:
        def gsb(name, shape, dt):
            return g.enter_context(nc.sbuf_tensor(un(name), list(shape), dt))

        PT = [g.enter_context(nc.psum_tensor("pt%d" % i, [128, 1024], BF16)) for i in range(2)]
        BPT = [Buf() for _ in range(2)]
        PS = [g.enter_context(nc.psum_tensor("ps%d" % i, [128, 512], F32)) for i in range(6)]
        BPS = [Buf() for _ in range(6)]
        ptc = [0]

        ident_f = gsb("ident_f", [128, 128], F32)
        ident = gsb("ident", [128, 128], BF16)
        B_identf, B_ident = Buf(), Buf()
        S.dma("sp", ident_f[:], c_ident, w=[B_identf], chan=B_identf)
        S.op("dve", lambda e: e.tensor_copy(out=ident[:], in_=ident_f[:]), r=[B_identf], w=[B_ident])

        def mm(out_, lhsT, rhs, start, stop, r, w, skip=False):
            S.op("pe", lambda e: e.matmul(out_, lhsT=lhsT, rhs=rhs, start=start, stop=stop,
                                          skip_group_check=skip), r=r, w=w)

        def tr(out_, in_, r, w):
            S.op("pe", lambda e: e.transpose(out=out_, in_=in_, identity=ident[:]), r=list(r) + [B_ident], w=w)

        def act(out_, in_, func, r, w, **kw):
            S.op("act", lambda e: e.activation(out=out_, in_=in_, func=func, **kw), r=r, w=w)

        def tt(eng, out_, in0, in1, op, r, w):
            S.op(eng, lambda e: e.tensor_tensor(out=out_, in0=in0, in1=in1, op=op), r=r, w=w)

        def ts(eng, out_, in0, s1, op0, r, w, s2=None, op1=None):
            if op1 is None:
                S.op(eng, lambda e: e.tensor_scalar(out=out_, in0=in0, scalar1=s1, scalar2=None, op0=op0), r=r, w=w)
            else:
                S.op(eng, lambda e: e.tensor_scalar(out=out_, in0=in0, scalar1=s1, scalar2=s2, op0=op0, op1=op1),
                     r=r, w=w)

        def stt(out_, in0, scalar, in1, op0, op1, r, w):
            S.op("dve", lambda e: e.scalar_tensor_tensor(out=out_, in0=in0, scalar=scalar, in1=in1, op0=op0, op1=op1),
                 r=r, w=w)

        def cp(eng, out_, in_, r, w):
            if eng == "act":
                act(out_, in_, AF.Copy, r, w)
            else:
                S.op(eng, lambda e: e.tensor_copy(out=out_, in_=in_), r=r, w=w)

        def recip(out_, in_, r, w):
            S.op("dve", lambda e: e.reciprocal(out=out_, in_=in_), r=r, w=w)

        def memset(eng, ap, val, w):
            S.op(eng, lambda e: e.memset(ap, val), w=w)

        hslot0 = gsb("hslot0", [128, 4, D], F32)
        pref = {"ready": False}
        D_H0 = Buf()
        mhalf = gsb("mhalf", [128, 4], F32)
        B_mhalf = Buf()
        memset("pool", mhalf[:], -0.5, [B_mhalf])

        def rstd_chain(ss, B_ss, tmp, B_tmp, rs, B_rs):
            ts("dve", tmp, ss, EPS, ALU.add, (B_ss if isinstance(B_ss, list) else [B_ss]), [B_tmp])
            if USE_POW:
                tt("pool", rs, tmp, mhalf[:], ALU.pow, [B_tmp, B_mhalf], [B_rs])
            else:
                act(tmp, tmp, AF.Sqrt, [B_tmp], [B_tmp])
                recip(rs, tmp, [B_tmp], [B_rs])

        Hv = H.rearrange("(t s p) d -> t p s d", s=4, p=128)
        Xv = x_in.rearrange("(t s p) d -> t p s d", s=4, p=128)
        Ov = out.rearrange("(t s p) d -> t p s d", s=4, p=128)

        class TileCtx:
            def __init__(self, es, nwsrc, src=None, nslots=2, nx=1):
                self.src = Hv if src is None else src
                sb = lambda name, shape, dt: es.enter_context(nc.sbuf_tensor(un(name), list(shape), dt))
                self.h = [hslot0] + [sb("h%d" % i, [128, 4, D], F32) for i in range(1, nslots)]
                self.Bh = [[Buf() for _ in range(4)] for _ in range(nslots)]
                self.nslots = nslots
                self.nwb = sb("nwb", [128, D], F32)
                self.B_nwb = Buf()
                S.dma("sp", self.nwb[:], nwsrc.partition_broadcast(128), w=[self.B_nwb], chan=self.B_nwb)
                self.xns = [sb("xn%d" % i, [128, D], BF16) for i in range(2)]
                self.B_xns = [Buf(), Buf()]
                self.nx = nx
                self.xnTs = [sb("xnT%d" % i, [128, 8, 512], BF16) for i in range(nx)]
                self.B_xnTs = [[Buf() for _ in range(8)] for _ in range(nx)]
                self.xnT, self.B_xnT = self.xnTs[0], self.B_xnTs[0]
                self.ss = sb("ss", [128, 4], F32)
                self.sst = sb("sst", [128, 4], F32)
                self.rs = sb("rs", [128, 4], F32)
                self.B_ss, self.B_sst, self.B_rs = [Buf() for _ in range(4)], Buf(), Buf()

            def use(self, t):
                self.xnT, self.B_xnT = self.xnTs[t % self.nx], self.B_xnTs[t % self.nx]

            def load(self, t):
                sl = t % self.nslots
                if t == 0 and pref["ready"]:
                    pref["ready"] = False
                    return
                S.dma("sp", self.h[sl][:], self.src[t], w=self.Bh[sl], chan=self.Bh[sl][0])

            def prefetch_next(self, t):
                if t == NT - 1 and (NT - 1) % self.nslots != 0:
                    S.dma("sp", hslot0[:], Hv[0], r=[D_H0], w=self.Bh[0], chan=self.Bh[0][0])
                    pref["ready"] = True

            def _stt(self, t, s):
                sl = t % self.nslots
                h, Bh = self.h[sl], self.Bh[sl]
                xn, B_xn = self.xns[s % 2], self.B_xns[s % 2]
                stt(xn[:], h[:, s, :], self.rs[:, s:s + 1], self.nwb[:], ALU.mult, ALU.mult,
                    [Bh[s], self.B_rs, self.B_nwb], [B_xn])

            def norm_stats(self, t):
                sl = t % self.nslots
                h, Bh = self.h[sl], self.Bh[sl]
                for s in range(4):
                    act(self.xns[s % 2][:], h[:, s, :], AF.Square, [Bh[s]], [self.B_xns[s % 2], self.B_ss[s]],
                        scale=1.0 / 32.0, accum_out=self.ss[:, s:s + 1])
                rstd_chain(self.ss[:], self.B_ss, self.sst[:], self.B_sst, self.rs[:], self.B_rs)
                self._stt(t, 0)

            def norm_piece(self, t, g):
                xnT, B_xnT = self.xnTs[t % self.nx], self.B_xnTs[t % self.nx]
                s, half = g // 2, g % 2
                xn, B_xn = self.xns[s % 2], self.B_xns[s % 2]
                if half == 1 and s < 3:
                    self._stt(t, s + 1)
                b = ptc[0] % 2
                ptc[0] += 1
                for k4 in range(4):
                    kc = half * 4 + k4
                    tr(PT[b][:, k4 * 128:(k4 + 1) * 128], xn[:, kc * 128:(kc + 1) * 128], [B_xn], [BPT[b]])
                eng = "act" if half == 0 else "dve"
                cp(eng, xnT[:, half * 4:(half + 1) * 4, s * 128:(s + 1) * 128],
                   PT[b][:, 0:512].rearrange("p (k t) -> p k t", k=4),
                   [BPT[b]], B_xnT[half * 4:(half + 1) * 4])

            def norm_T(self, t):
                self.norm_stats(t)
                for g in range(8):
                    self.norm_piece(t, g)

        def load_w_cols(es, name, src2d, K, N, blocks):
            kc_n = K // 128
            wt = es.enter_context(nc.sbuf_tensor(un(name), [128, kc_n, N], BF16))
            srcv = src2d.rearrange("(k p) n -> p k n", p=128)
            Bw = {}
            for (c0, c1) in blocks:
                b = Buf()
                for c in range(c0, c1, 128):
                    Bw[c] = b
                S.dma("pool", wt[:, :, c0:c1], srcv[:, :, c0:c1], w=[b], chan=b)
            return wt, (lambda col: Bw[(col // 128) * 128])

        def load_w(es, name, src2d, K, N, ncols_split=1):
            kc_n = K // 128
            wt = es.enter_context(nc.sbuf_tensor(un(name), [128, kc_n, N], BF16))
            Bw = [Buf() for _ in range(kc_n)]
            for kc in range(kc_n):
                S.dma("pool", wt[:, kc, :], src2d[kc * 128:(kc + 1) * 128, :], w=[Bw[kc]], chan=Bw[kc])
            return wt, Bw


        def ffn_stage(i, j, first):
            with ExitStack() as es:
                sb = lambda name, shape, dt: es.enter_context(nc.sbuf_tensor(un(name), list(shape), dt))
                blocks = []
                for (c0, c1) in [(0, 1), (1, 2), (2, 4)] + [(c, min(c + 4, NFC)) for c in range(4, NFC, 4)]:
                    blocks.append((c0 * 128, c1 * 128))
                    blocks.append((DFF + c0 * 128, DFF + c1 * 128))
                win, Bwin = load_w_cols(es, "win", ffn_w_in[i, j], D, 2 * DFF, blocks)
                wout, Bwout = load_w(es, "wout", ffn_w_out[i, j], DFF, D)
                tc = TileCtx(es, norm_w[i, 2 * j], src=Xv if first else None)
                hid = sb("hid", [128, NFC, 512], BF16)
                B_hid = [Buf() for _ in range(NFC)]
                sg = [sb("sg%d" % k, [128, 512], F32) for k in range(2)]
                B_sg = [Buf(), Buf()]
                tc.load(0)
                tc.norm_T(0)
                for t in range(NT):
                    if t + 1 < NT:
                        tc.load(t + 1)
                    tc.prefetch_next(t)
                    sl = t % 2
                    for c in range(NFC):
                        pg, pu = (0, 1) if c % 2 == 0 else (2, 3)
                        for kc in range(8):
                            mm(PS[pg][:], win[:, kc, c * 128:(c + 1) * 128], tc.xnT[:, kc, :], kc == 0, kc == 7,
                               [Bwin(c * 128), tc.B_xnT[kc]], [BPS[pg]])
                        for kc in range(8):
                            mm(PS[pu][:], win[:, kc, DFF + c * 128:DFF + (c + 1) * 128], tc.xnT[:, kc, :],
                               kc == 0, kc == 7, [Bwin(DFF + c * 128), tc.B_xnT[kc]], [BPS[pu]])
                        k = c % 2
                        act(sg[k][:], PS[pg][:], AF.Silu, [BPS[pg]], [B_sg[k]])
                        tt("dve", hid[:, c, :], sg[k][:], PS[pu][:], ALU.mult, [B_sg[k], BPS[pu]], [B_hid[c]])
                        if c == NFC // 2 and t + 1 < NT:
                            tc.norm_stats(t + 1)
                    q = 0
                    for s in range(4):
                        for half in range(2):
                            if t + 1 < NT:
                                tc.norm_piece(t + 1, q)
                            b = 4 + (q % 2)
                            q += 1
                            for c in range(NFC):
                                mm(PS[b][:], hid[:, c, s * 128:(s + 1) * 128], wout[:, c, half * 512:(half + 1) * 512],
                                   c == 0, c == NFC - 1, [B_hid[c], Bwout[c]], [BPS[b]])
                            hv = tc.h[sl][:, s, half * 512:(half + 1) * 512]
                            stt(hv, PS[b][:], 0.5, hv, ALU.mult, ALU.add, [BPS[b], tc.Bh[sl][s]], [tc.Bh[sl][s]])
                    S.dma("sp", Hv[t], tc.h[sl][:], r=tc.Bh[sl], w=([D_H0] if t == 0 else []), chan=tc.Bh[sl][0])
            S.barrier()

        def ple_stage(i, last):
            with ExitStack() as es:
                sb = lambda name, shape, dt: es.enter_context(nc.sbuf_tensor(un(name), list(shape), dt))
                wg, Bwg = load_w(es, "wg", ple_w_gate[i], D, D)
                wp, Bwp = load_w(es, "wp", ple_w_proj[i], DPLE, D)
                tc = TileCtx(es, norm_w[i, 3], nslots=3, nx=2)
                pv = p_in[i].rearrange("(t s p) d -> t p s d", s=4, p=128)
                pb = [sb("pb%d" % k, [128, 4, DPLE], BF16) for k in range(2)]
                B_pb = [Buf(), Buf()]
                pTs = [sb("pT%d" % k, [128, 2, 512], BF16) for k in range(2)]
                B_pTs = [Buf(), Buf()]

                def ptrans(t):
                    for k2 in range(2):
                        b = ptc[0] % 2
                        ptc[0] += 1
                        for s in range(4):
                            tr(PT[b][:, s * 128:(s + 1) * 128], pb[t % 2][:, s, k2 * 128:(k2 + 1) * 128],
                               [B_pb[t % 2]], [BPT[b]])
                        cp("act", pTs[t % 2][:, k2, :], PT[b][:, 0:512], [BPT[b]], [B_pTs[t % 2]])
                sgt = [sb("sgt%d" % k, [128, 512], F32) for k in range(4)]
                B_sgt = [Buf() for _ in range(4)]
                if last:
                    fnw = sb("fnw", [128, D], F32)
                    B_fnw = Buf()
                    S.dma("sp", fnw[:], final_norm_w.partition_broadcast(128), w=[B_fnw], chan=B_fnw)
                    junk = sb("junk", [128, D], BF16)
                    B_junk = Buf()
                    fss = sb("fss", [128, 4], F32)
                    fst = sb("fst", [128, 4], F32)
                    frs = sb("frs", [128, 4], F32)
                    B_fss, B_fst, B_frs = Buf(), Buf(), Buf()
                tc.load(0)
                S.dma("pool", pb[0][:], pv[0], w=[B_pb[0]], chan=B_pb[0])
                if NT > 1:
                    tc.load(1)
                tc.norm_T(0)
                if NT > 1:
                    tc.norm_stats(1)
                for t in range(NT):
                    if t + 2 < NT:
                        tc.load(t + 2)
                    if t + 1 < NT:
                        S.dma("pool", pb[(t + 1) % 2][:], pv[t + 1], w=[B_pb[(t + 1) % 2]], chan=B_pb[(t + 1) % 2])
                    if not last:
                        tc.prefetch_next(t)
                    sl = t % 3
                    psl = t % 2
                    tc.use(t)
                    if t == 0:
                        ptrans(0)
                    pT, B_pT = pTs[t % 2], B_pTs[t % 2]
                    q = 0
                    for s in range(4):
                        for half in range(2):
                            if t + 1 < NT:
                                tc.norm_piece(t + 1, q)
                                if q == 5:
                                    ptrans(t + 1)
                            bg, bp = ((0, 1), (2, 3), (4, 5))[q % 3]
                            k = q % 4
                            q += 1
                            for kc in range(8):
                                mm(PS[bg][:], tc.xnT[:, kc, s * 128:(s + 1) * 128], wg[:, kc, half * 512:(half + 1) * 512],
                                   kc == 0, kc == 7, [tc.B_xnT[kc], Bwg[kc]], [BPS[bg]])
                            for kc in range(2):
                                mm(PS[bp][:], pT[:, kc, s * 128:(s + 1) * 128], wp[:, kc, half * 512:(half + 1) * 512],
                                   kc == 0, kc == 1, [B_pT, Bwp[kc]], [BPS[bp]])
                            act(sgt[k][:], PS[bg][:], AF.Sigmoid, [BPS[bg]], [B_sgt[k]])
                            tt("dve", sgt[k][:], sgt[k][:], PS[bp][:], ALU.mult, [B_sgt[k], BPS[bp]], [B_sgt[k]])
                            hv = tc.h[sl][:, s, half * 512:(half + 1) * 512]
                            tt("pool" if half == 0 else "dve", hv, hv, sgt[k][:], ALU.add, [B_sgt[k], tc.Bh[sl][s]],
                               [tc.Bh[sl][s]])
                    if t + 2 < NT:
                        tc.norm_stats(t + 2)
                    if last:
                        for s in range(4):
                            act(junk[:], tc.h[sl][:, s, :], AF.Square, [tc.Bh[sl][s]], [B_junk, B_fss],
                                scale=1.0 / 32.0, accum_out=fss[:, s:s + 1])
                        rstd_chain(fss[:], B_fss, fst[:], B_fst, frs[:], B_frs)
                        for s in range(4):
                            stt(tc.h[sl][:, s, :], tc.h[sl][:, s, :], frs[:, s:s + 1], fnw[:], ALU.mult, ALU.mult,
                                [tc.Bh[sl][s], B_frs, B_fnw], [tc.Bh[sl][s]])
                        S.dma("sp", Ov[t], tc.h[sl][:], r=tc.Bh[sl], chan=tc.Bh[sl][0])
                    else:
                        S.dma("sp", Hv[t], tc.h[sl][:], r=tc.Bh[sl], w=([D_H0] if t == 0 else []), chan=tc.Bh[sl][0])
            S.barrier()

        def ret_stage1(i):
            with ExitStack() as es:
                sb = lambda name, shape, dt: es.enter_context(nc.sbuf_tensor(un(name), list(shape), dt))
                win, Bwin = load_w_cols(es, "rwin", ret_w_in, D, RET_IN,
                                        [(0, 128), (128, 256), (256, 512)] +
                                        [(c, c + 512) for c in range(512, RET_IN, 512)])
                tc = TileCtx(es, norm_w[i, 1], nx=2)
                rope = [sb("rope%d" % k, [128, 2, 512], F32) for k in range(2)]
                B_rope = [Buf(), Buf()]
                ropev = c_rope.rearrange("f p t -> p f t")
                qk = sb("qk", [128, 16, 512], BF16)
                B_qk = [Buf() for _ in range(16)]
                tmp = [sb("rt%d" % k, [128, 512], F32) for k in range(4)]
                B_tmp = [Buf() for _ in range(4)]
                vg = [sb("vg%d" % k, [128, 4096], BF16) for k in range(2)]
                B_vg = [Buf(), Buf()]
                QKv = QKs.rearrange("(c p) t -> p c t", p=128)
                tc.load(0)
                tc.norm_T(0)
                S.dma("sp", rope[0][:], ropev[:, :, 0:512], w=[B_rope[0]], chan=B_rope[0])
                nvg = 0
                for t in range(NT):
                    if t + 1 < NT:
                        tc.load(t + 1)
                        S.dma("sp", rope[(t + 1) % 2][:], ropev[:, :, (t + 1) * 512:(t + 2) * 512],
                              w=[B_rope[(t + 1) % 2]], chan=B_rope[(t + 1) % 2])
                    sl = t % 2
                    tc.use(t)
                    for pr in range(8):
                        if pr == 2 and t + 1 < NT:
                            tc.norm_stats(t + 1)
                        isk = pr >= 4
                        oc1, oc2 = 2 * pr, 2 * pr + 1
                        b1, b2 = (0, 1) if pr % 2 == 0 else (2, 3)
                        for (oc, b) in ((oc1, b1), (oc2, b2)):
                            for kc in range(8):
                                mm(PS[b][:], win[:, kc, oc * 128:(oc + 1) * 128], tc.xnT[:, kc, :], kc == 0, kc == 7,
                                   [Bwin(oc * 128), tc.B_xnT[kc]], [BPS[b]])
                        cos = rope[sl][:, 0, :]
                        sin = rope[sl][:, 1, :]

                        def rmul(o_, p_, tab, r_, w_, isk=isk):
                            if isk:
                                stt(o_, p_, float(RET_DK ** -0.5), tab, ALU.mult, ALU.mult, r_, w_)
                            else:
                                tt("dve", o_, p_, tab, ALU.mult, r_, w_)
                        rmul(tmp[0][:], PS[b1][:], cos, [BPS[b1], B_rope[sl]], [B_tmp[0]])
                        rmul(tmp[1][:], PS[b2][:], sin, [BPS[b2], B_rope[sl]], [B_tmp[1]])
                        tt("pool", qk[:, oc1, :], tmp[0][:], tmp[1][:], ALU.subtract, [B_tmp[0], B_tmp[1]], [B_qk[oc1]])
                        rmul(tmp[2][:], PS[b1][:], sin, [BPS[b1], B_rope[sl]], [B_tmp[2]])
                        rmul(tmp[3][:], PS[b2][:], cos, [BPS[b2], B_rope[sl]], [B_tmp[3]])
                        tt("pool", qk[:, oc2, :], tmp[2][:], tmp[3][:], ALU.add, [B_tmp[2], B_tmp[3]], [B_qk[oc2]])
                    S.dma("sp", QKv[:, :, t * 512:(t + 1) * 512], qk[:], r=B_qk, chan=B_qk[0])
                    q = 0
                    for s in range(4):
                        k = nvg % 2
                        nvg += 1
                        for n in range(8):
                            if n % 4 == 0 and t + 1 < NT:
                                tc.norm_piece(t + 1, 2 * s + n // 4)
                            b = 4 + (q % 2)
                            q += 1
                            for kc in range(8):
                                mm(PS[b][:], tc.xnT[:, kc, s * 128:(s + 1) * 128],
                                   win[:, kc, 2048 + n * 512:2048 + (n + 1) * 512], kc == 0, kc == 7,
                                   [tc.B_xnT[kc], Bwin(2048 + n * 512)], [BPS[b]])
                            act(vg[k][:, n * 512:(n + 1) * 512], PS[b][:], AF.Copy if n < 4 else AF.Silu,
                                [BPS[b]], [B_vg[k]])
                        r0 = t * 512 + s * 128
                        S.dma("sp", VGs[r0:r0 + 128, :], vg[k][:], r=[B_vg[k]], chan=B_vg[k])
            S.barrier()

        def ret_stage2(i):
            with ExitStack() as es:
                sb = lambda name, shape, dt: es.enter_context(nc.sbuf_tensor(un(name), list(shape), dt))
                wout, Bwout = load_w(es, "rwout", ret_w_out, RET_H * RET_DV, D)
                dmask = sb("dmask", [128, 4, 128], F32)
                xi = sb("xi", [128, 8, 128], F32)
                zeta = sb("zeta", [128, 1024], F32)
                gnw = sb("gnw", [128, 2048], F32)
                B_c = Buf()
                S.dma("sp", dmask[:], c_dmask, w=[B_c], chan=B_c)
                B_c2 = Buf()
                S.dma("sp", xi[:], c_xi, w=[B_c2], chan=B_c2)
                B_c3 = Buf()
                S.dma("sp", zeta[:], c_zeta, w=[B_c3], chan=B_c3)
                B_c4 = Buf()
                S.dma("sp", gnw[:], ret_gn_w.partition_broadcast(128), w=[B_c4], chan=B_c4)
                Sf = sb("Sf", [128, 8, 512], F32)
                Sb_ = sb("Sb", [128, 8, 512], BF16)
                B_Sf = [Buf() for _ in range(8)]
                B_Sb = [Buf() for _ in range(8)]
                for k in range(8):
                    memset("pool", Sf[:, k, :], 0.0, [B_Sf[k]])
                    memset("pool", Sb_[:, k, :], 0.0, [B_Sb[k]])
                NQ = 3
                qk = [sb("qkc%d" % k, [128, 16, 128], BF16) for k in range(NQ)]
                B_qk = [Buf() for _ in range(NQ)]
                vg = [sb("vgc%d" % k, [128, 4096], BF16) for k in range(NQ)]
                B_vg = [Buf() for _ in range(NQ)]
                NHC = 4
                hc = [sb("hc%d" % k, [128, D], F32) for k in range(NHC)]
                B_hc = [Buf() for _ in range(NHC)]
                sTm = [sb("sTm%d" % k, [128, 4, 128], BF16) for k in range(2)]
                B_sTm = [Buf(), Buf()]
                qx = [sb("qx%d" % k, [128, 8, 128], BF16) for k in range(2)]
                B_qx = [Buf(), Buf()]
                kz = [sb("kz%d" % k, [128, 1024], BF16) for k in range(2)]
                B_kz = [Buf(), Buf()]
                ot = [sb("ot%d" % k, [128, 512], F32) for k in range(2)]
                B_ot = [Buf(), Buf()]
                ys = [sb("y%d" % k, [128, 2048], BF16) for k in range(2)]
                B_ys = [[Buf() for _ in range(4)] for _ in range(2)]
                yT = sb("yT", [128, 16, 128], BF16)
                B_yT = [Buf() for _ in range(4)]
                junk = sb("rjunk", [128, 512], BF16)
                B_junk = Buf()
                gss = sb("gss", [128, 4], F32)
                gst = sb("gst", [128, 4], F32)
                grs = sb("grs", [128, 4], F32)
                B_gss, B_gst, B_grs = Buf(), Buf(), Buf()
                QKv = QKs.rearrange("(c p) t -> p c t", p=128)
                Hc = H.rearrange("(n p) d -> n p d", p=128)

                def load(c):
                    k = c % NQ
                    S.dma("sp", qk[k][:], QKv[:, :, c * 128:(c + 1) * 128], w=[B_qk[k]], chan=B_qk[k])
                    S.dma("sp", vg[k][:], VGs[c * 128:(c + 1) * 128, :], w=[B_vg[k]], chan=B_vg[k])
                    S.dma("sp", hc[c % NHC][:], Hc[c], w=[B_hc[c % NHC]], chan=B_hc[c % NHC])

                def phaseA(c):
                    k = c % 2
                    kq = c % NQ
                    for h in range(4):
                        qA, qB = qk[kq][:, 2 * h, :], qk[kq][:, 2 * h + 1, :]
                        kA, kB = qk[kq][:, 8 + 2 * h, :], qk[kq][:, 8 + 2 * h + 1, :]
                        mm(PS[4][:, h * 128:(h + 1) * 128], kA, qA, True, False, [B_qk[kq]], [BPS[4]])
                        mm(PS[4][:, h * 128:(h + 1) * 128], kB, qB, False, True, [B_qk[kq]], [BPS[4]])
                    b = ptc[0] % 2
                    ptc[0] += 1
                    for kc in range(8):
                        tr(PT[b][:, kc * 128:(kc + 1) * 128], qk[kq][:, 8 + kc, :], [B_qk[kq]], [BPT[b]])
                    tt("dve", sTm[k][:], PS[4][:].rearrange("p (h t) -> p h t", h=4), dmask[:], ALU.mult,
                       [BPS[4], B_c], [B_sTm[k]])
                    tt("dve", kz[k][:], PT[b][:], zeta[:], ALU.mult, [BPT[b], B_c3], [B_kz[k]])
                    tt("pool", qx[k][:], qk[kq][:, 0:8, :], xi[:], ALU.mult, [B_qk[kq], B_c2], [B_qx[k]])

                def phaseC(c):
                    k = c % 2
                    kq = c % NQ
                    n = 0
                    for h in range(4):
                        vh = vg[kq][:, h * 512:(h + 1) * 512]
                        mm(PS[h][:], sTm[k][:, h, :], vh, True, False, [B_sTm[k], B_vg[kq]], [BPS[h]])
                        mm(PS[h][:], qx[k][:, 2 * h, :], Sb_[:, 2 * h, :], False, False, [B_qx[k], B_Sb[2 * h]], [BPS[h]])
                        mm(PS[h][:], qx[k][:, 2 * h + 1, :], Sb_[:, 2 * h + 1, :], False, True,
                           [B_qx[k], B_Sb[2 * h + 1]], [BPS[h]])
                        for dc in range(2):
                            b = 4 + (n % 2)
                            n += 1
                            sidx = 2 * h + dc
                            mm(PS[b][:], kz[k][:, sidx * 128:(sidx + 1) * 128], vh, True, True,
                               [B_kz[k], B_vg[kq]], [BPS[b]])
                            stt(Sf[:, sidx, :], Sf[:, sidx, :], gamma_c[h], PS[b][:], ALU.mult, ALU.add,
                                [BPS[b], B_Sf[sidx]], [B_Sf[sidx]])
                            cp("act", Sb_[:, sidx, :], Sf[:, sidx, :], [B_Sf[sidx]], [B_Sb[sidx]])

                def phaseD(c):
                    k = c % 2
                    kq = c % NQ
                    y, B_y = ys[c % 2], B_ys[c % 2]
                    for h in range(4):
                        act(junk[:], PS[h][:], AF.Square, [BPS[h]], [B_junk, B_gss],
                            scale=float(1.0 / np.sqrt(512.0)), accum_out=gss[:, h:h + 1])
                    rstd_chain(gss[:], B_gss, gst[:], B_gst, grs[:], B_grs)
                    for h in range(4):
                        o2 = h % 2
                        stt(ot[o2][:], PS[h][:], grs[:, h:h + 1], gnw[:, h * 512:(h + 1) * 512], ALU.mult, ALU.mult,
                            [BPS[h], B_grs, B_c4], [B_ot[o2]])
                        tt("pool", y[:, h * 512:(h + 1) * 512], ot[o2][:],
                           vg[kq][:, 2048 + h * 512:2048 + (h + 1) * 512],
                           ALU.mult, [B_ot[o2], B_vg[kq]], [B_y[h]])

                def phaseB(c):
                    k3 = c % NHC
                    y, B_y = ys[c % 2], B_ys[c % 2]
                    for gq in range(4):
                        b = ptc[0] % 2
                        ptc[0] += 1
                        for k4 in range(4):
                            kc = gq * 4 + k4
                            tr(PT[b][:, k4 * 128:(k4 + 1) * 128], y[:, kc * 128:(kc + 1) * 128], [B_y[gq]], [BPT[b]])
                        cp("act", yT[:, gq * 4:(gq + 1) * 4, :], PT[b][:, 0:512].rearrange("p (k t) -> p k t", k=4),
                           [BPT[b]], [B_yT[gq]])
                    for half in range(2):
                        b = 4 + half
                        for kc in range(16):
                            mm(PS[b][:], yT[:, kc, :], wout[:, kc, half * 512:(half + 1) * 512], kc == 0, kc == 15,
                               [B_yT[kc // 4], Bwout[kc]], [BPS[b]])
                        hv = hc[k3][:, half * 512:(half + 1) * 512]
                        tt("dve", hv, hv, PS[b][:], ALU.add, [BPS[b], B_hc[k3]], [B_hc[k3]])
                    S.dma("sp", Hc[c], hc[k3][:], r=[B_hc[k3]], w=([D_H0] if c < 4 else []), chan=B_hc[k3])

                load(0)
                if NS > 1:
                    load(1)
                phaseA(0)
                for c in range(NS):
                    phaseC(c)
                    if c + 2 < NS:
                        load(c + 2)
                    if c + 1 < NS:
                        phaseA(c + 1)
                    phaseD(c)
                    if c > 0:
                        phaseB(c - 1)
                    if c == NS - 1 and NS > 5:
                        B_p0 = Buf()
                        S.dma("sp", hslot0[:], Hv[0], r=[D_H0], w=[B_p0], chan=B_p0)
                        pref["ready"] = True
                phaseB(NS - 1)
            S.barrier()

        def fox_stage1(i):
            with ExitStack() as es:
                sb = lambda name, shape, dt: es.enter_context(nc.sbuf_tensor(un(name), list(shape), dt))
                win, Bwin = load_w_cols(es, "fwin", fox_w_in, D, FOX_IN,
                                        [(0, 128), (128, 256), (256, 512)] +
                                        [(c, c + 512) for c in range(512, 3072, 512)] + [(3072, 3088)])
                tc = TileCtx(es, norm_w[i, 1], nx=2)
                negb = sb("negb", [16, 1], F32)
                B_negb = Buf()
                S.dma("sp", negb[:], fox_b_f, w=[B_negb], chan=B_negb)
                ts("dve", negb[:], negb[:], -1.0, ALU.mult, [B_negb], [B_negb])
                qs = [sb("qs%d" % k, [128, 8, 512], BF16) for k in range(2)]
                B_qs = [Buf(), Buf()]
                vs = [sb("vs%d" % k, [128, 1024], BF16) for k in range(2)]
                B_vs = [Buf(), Buf()]
                ef = sb("ef", [16, 512], F32)
                B_ef = Buf()
                lf = [sb("lf%d" % k, [16, 512], F32) for k in range(2)]
                B_lf = [Buf(), Buf()]
                ones = sb("ones", [16, 512], F32)
                CS = sb("CS", [16, T], F32)
                r1 = sb("r1", [16, 512], F32)
                P3 = sb("P3", [16, 3, 512], BF16)
                QC = [sb("QC%d" % k, [16, 3, 512], BF16) for k in range(2)]
                KC = [sb("KC%d" % k, [16, 3, 512], BF16) for k in range(2)]
                C1 = sb("C1", [16, 3, 512], BF16)
                C8 = sb("C8", [16, 3, 512], BF16)
                B_ones, B_CS, B_r1, B_P3, B_C1, B_C8 = Buf(), Buf(), Buf(), Buf(), Buf(), Buf()
                B_QC, B_KC = [Buf(), Buf()], [Buf(), Buf()]
                memset("pool", ones[:], 1.0, [B_ones])
                memset("pool", C1[:], 0.125, [B_C1])
                memset("pool", C8[:], 8.0, [B_C8])
                QAv = QA.rearrange("(m two) r t -> two r m t", two=2)
                KAv = KA.rearrange("(m two) r t -> two r m t", two=2)
                Vv = Vs.rearrange("h p n d -> p n h d")
                tc.load(0)
                tc.norm_T(0)
                nv = 0
                pending = []
                for t in range(NT):
                    if t + 1 < NT:
                        tc.load(t + 1)
                    tc.use(t)
                    q = 0
                    for which in range(2):
                        for m in range(8):
                            if which == 0 and m == 2 and t + 1 < NT:
                                tc.norm_stats(t + 1)
                            b = q % 4
                            q += 1
                            oc = which * 8 + m
                            for kc in range(8):
                                mm(PS[b][:], win[:, kc, oc * 128:(oc + 1) * 128], tc.xnT[:, kc, :], kc == 0, kc == 7,
                                   [Bwin(oc * 128), tc.B_xnT[kc]], [BPS[b]])
                            if m % 2 == 0:
                                act(qs[which][:, m, :], PS[b][:], AF.Copy, [BPS[b]], [B_qs[which]],
                                    scale=(0.125 if which == 0 else 1.0))
                            else:
                                ts("dve", qs[which][:, m, :], PS[b][:], (0.125 if which == 0 else 1.0), ALU.mult,
                                   [BPS[b]], [B_qs[which]])
                            if pending:
                                pending.pop(0)()
                        dst = QAv if which == 0 else KAv
                        for two in range(2):
                            S.dma("sp", dst[two][0:64, :, t * 512:(t + 1) * 512], qs[which][two * 64:(two + 1) * 64, :, :],
                                  r=[B_qs[which]], chan=B_qs[which])
                    for s in range(4):
                        k = nv % 2
                        nv += 1
                        for half in range(2):
                            if t + 1 < NT:
                                tc.norm_piece(t + 1, 2 * s + half)
                            b = 4 + half
                            for kc in range(8):
                                mm(PS[b][:], tc.xnT[:, kc, s * 128:(s + 1) * 128],
                                   win[:, kc, 2048 + half * 512:2048 + (half + 1) * 512], kc == 0, kc == 7,
                                   [tc.B_xnT[kc], Bwin(2048 + half * 512)], [BPS[b]])
                            cp("act" if half == 0 else "dve", vs[k][:, half * 512:(half + 1) * 512], PS[b][:],
                               [BPS[b]], [B_vs[k]])
                        S.dma("sp", Vv[:, t * 4 + s, :, :], vs[k][:].rearrange("p (h d) -> p h d", h=16),
                              r=[B_vs[k]], chan=B_vs[k])
                    for kc in range(8):
                        mm(PS[0][0:16, :], win[:, kc, 3072:3088], tc.xnT[:, kc, :], kc == 0, kc == 7,
                           [Bwin(3072), tc.B_xnT[kc]], [BPS[0]])
                    act(ef[:], PS[0][0:16, :], AF.Exp, [BPS[0], B_negb], [B_ef], scale=-1.0, bias=negb[:])
                    k2 = t % 2
                    act(lf[k2][:], ef[:], AF.Ln, [B_ef], [B_lf[k2]], bias=1.0)
                    cst = CS[:, t * 512:(t + 1) * 512]
                    init = 0.0 if t == 0 else CS[:, t * 512 - 1:t * 512]
                    S.op("dve", lambda e, cst=cst, init=init, k2=k2: e.tensor_tensor_scan(
                        out=cst, data0=ones[:], data1=lf[k2][:], initial=init, op0=ALU.mult, op1=ALU.add),
                        r=[B_ones, B_lf[k2], B_CS], w=[B_CS])
                    tsl = slice(t * 512, (t + 1) * 512)

                    def chain(cst=cst, k2=k2, tsl=tsl):
                        return [
                            lambda: cp("dve", P3[:, 0, :], cst, [B_CS], [B_P3]),
                            lambda: tt("dve", r1[:], cst, P3[:, 0, :], ALU.subtract, [B_CS, B_P3], [B_r1]),
                            lambda: cp("dve", P3[:, 1, :], r1[:], [B_r1], [B_P3]),
                            lambda: tt("dve", r1[:], r1[:], P3[:, 1, :], ALU.subtract, [B_r1, B_P3], [B_r1]),
                            lambda: cp("dve", P3[:, 2, :], r1[:], [B_r1], [B_P3]),
                            lambda: ts("dve", QC[k2][:], P3[:], -0.125, ALU.mult, [B_P3], [B_QC[k2]]),
                            lambda: ts("dve", KC[k2][:], P3[:], 8.0, ALU.mult, [B_P3], [B_KC[k2]]),
                            lambda: (S.dma("sp", QA[:, 64:67, tsl], QC[k2][:], r=[B_QC[k2]], chan=B_QC[k2]),
                                     S.dma("sp", KA[:, 67:70, tsl], KC[k2][:], r=[B_KC[k2]], chan=B_KC[k2]),
                                     S.dma("sp", QA[:, 67:70, tsl], C1[:], r=[B_C1], chan=B_C1),
                                     S.dma("sp", KA[:, 64:67, tsl], C8[:], r=[B_C8], chan=B_C8)),
                        ]
                    pending.extend(chain())
                while pending:
                    pending.pop(0)()
            S.barrier()

        def fox_stage3(O, B_O):
            with ExitStack() as es:
                sb = lambda name, shape, dt: es.enter_context(nc.sbuf_tensor(un(name), list(shape), dt))
                mneg_f = sb("mneg_f", [128, 128], F32)
                mneg = sb("mneg", [128, 128], BF16)
                B_mf, B_m = Buf(), Buf()
                S.dma("sp", mneg_f[:], c_maskneg, w=[B_mf], chan=B_mf)
                cp("dve", mneg[:], mneg_f[:], [B_mf], [B_m])
                qa = [sb("qa%d" % k, [128, T], BF16) for k in range(2)]
                ka = [sb("ka%d" % k, [128, T], BF16) for k in range(2)]
                va = [sb("va%d" % k, [128, NS, 65], BF16) for k in range(2)]
                B_qa, B_ka, B_va = [Buf(), Buf()], [Buf(), Buf()], [Buf(), Buf()]
                for k in range(2):
                    memset("pool", va[k][:, :, 64:65], 1.0, [B_va[k]])
                NPT = 4
                pt = [sb("ptx%d" % k, [128, 512], BF16) for k in range(NPT)]
                B_pt = [Buf() for _ in range(NPT)]
                rl = sb("rl", [128, 4], F32)
                B_rl = Buf()

                def load(h):
                    k = h % 2
                    S.dma("sp", qa[k][0:70, :], QA[h], w=[B_qa[k]], chan=B_qa[k])
                    S.dma("sp", ka[k][0:70, :], KA[h], w=[B_ka[k]], chan=B_ka[k])
                    S.dma("sp", va[k][:, :, 0:64], Vs[h], w=[B_va[k]], chan=B_va[k])
                load(0)
                B_p0 = Buf()
                S.dma("sp", hslot0[:], Hv[0], w=[B_p0], chan=B_p0)
                pref["ready"] = True
                items = [(h, iq, j) for h in range(FOX_H) for iq in range(NT) for j in range(4 * iq + 4)]
                LAG = 2

                def score(idx):
                    h, iq, j = items[idx]
                    k = h % 2
                    sb_ = idx % NPT
                    jj = j - 4 * iq
                    c0 = max(jj, 0) * 128
                    mm(PS[sb_][:, c0:512], ka[k][0:70, j * 128:(j + 1) * 128],
                       qa[k][0:70, iq * 512 + c0:(iq + 1) * 512], True, True,
                       [B_ka[k], B_qa[k]], [BPS[sb_]])
                    if jj >= 0:
                        mm(PS[sb_][:, c0:c0 + 128], ident[:], mneg[:], False, True, [B_ident, B_m], [BPS[sb_]],
                           skip=True)
                    act(pt[sb_][:, c0:512], PS[sb_][:, c0:512], AF.Exp, [BPS[sb_]], [B_pt[sb_]])

                def pv(idx):
                    h, iq, j = items[idx]
                    k = h % 2
                    sb_ = idx % NPT
                    jj = j - 4 * iq
                    ob = 4 + ((h * NT + iq) % 2)
                    if iq == 0 and j == 0 and h + 1 < FOX_H:
                        load(h + 1)
                    for qs_ in range(max(jj, 0), 4):
                        last = (j == 4 * iq + qs_)
                        mm(PS[ob][:, qs_ * 65:(qs_ + 1) * 65], pt[sb_][:, qs_ * 128:(qs_ + 1) * 128],
                           va[k][:, j, :], (j == 0 and qs_ == 0), last, [B_pt[sb_], B_va[k]], [BPS[ob]],
                           skip=True)
                    if j == 4 * iq + 3:
                        ov = PS[ob][:, 0:260].rearrange("p (a b) -> p a b", b=65)
                        recip(rl[:], ov[:, :, 64], [BPS[ob]], [B_rl])
                        for qs_ in range(4):
                            ts("dve", O[:, iq * 4 + qs_, h * 64:(h + 1) * 64], PS[ob][:, qs_ * 65:qs_ * 65 + 64],
                               rl[:, qs_:qs_ + 1], ALU.mult, [BPS[ob], B_rl], [B_O])

                for idx in range(len(items) + LAG):
                    if idx < len(items):
                        score(idx)
                    if idx >= LAG:
                        pv(idx - LAG)
            S.barrier()

        def fox_stage4(i, O, B_O):
            with ExitStack() as es:
                sb = lambda name, shape, dt: es.enter_context(nc.sbuf_tensor(un(name), list(shape), dt))
                wo, Bwo = load_w(es, "fwo", fox_w_out, D, D)
                h = [hslot0, sb("fh1", [128, 4, D], F32)]
                Bh = [Buf(), Buf()]
                oT = sb("oT", [128, 8, 128], BF16)
                B_oT = [Buf(), Buf()]
                if pref["ready"]:
                    pref["ready"] = False
                else:
                    S.dma("sp", h[0][:], Hv[0], w=[Bh[0]], chan=Bh[0])
                q = 0
                for t in range(NT):
                    if t + 1 < NT:
                        S.dma("sp", h[(t + 1) % 2][:], Hv[t + 1], w=[Bh[(t + 1) % 2]], chan=Bh[(t + 1) % 2])
                    sl = t % 2
                    if t == NT - 1 and (NT - 1) % 2 != 0:
                        S.dma("sp", hslot0[:], Hv[0], r=[D_H0], w=[Bh[0]], chan=Bh[0])
                        pref["ready"] = True
                    for s in range(4):
                        n = t * 4 + s
                        for half in range(2):
                            b = ptc[0] % 2
                            ptc[0] += 1
                            for k4 in range(4):
                                kc = half * 4 + k4
                                tr(PT[b][:, k4 * 128:(k4 + 1) * 128], O[:, n, kc * 128:(kc + 1) * 128], [B_O], [BPT[b]])
                            cp("act", oT[:, half * 4:(half + 1) * 4, :],
                               PT[b][:, 0:512].rearrange("p (k t) -> p k t", k=4), [BPT[b]], [B_oT[half]])
                        for half in range(2):
                            b = q % 4
                            q += 1
                            for kc in range(8):
                                mm(PS[b][:], oT[:, kc, :], wo[:, kc, half * 512:(half + 1) * 512], kc == 0, kc == 7,
                                   [B_oT[kc // 4], Bwo[kc]], [BPS[b]])
                            hv = h[sl][:, s, half * 512:(half + 1) * 512]
                            tt("dve", hv, hv, PS[b][:], ALU.add, [BPS[b], Bh[sl]], [Bh[sl]])
                    S.dma("sp", Hv[t], h[sl][:], r=[Bh[sl]], w=([D_H0] if t == 0 else []), chan=Bh[sl])
            S.barrier()

        for i in range(DEPTH):
            ffn_stage(i, 0, first=(i == 0))
            if i % 2 == 0:
                ret_stage1(i)
                ret_stage2(i)
            else:
                fox_stage1(i)
                with ExitStack() as fs:
                    O = fs.enter_context(nc.sbuf_tensor("O", [128, NS, D], BF16))
                    B_O = Buf()
                    fox_stage3(O, B_O)
                    fox_stage4(i, O, B_O)
            ffn_stage(i, 1, first=False)
            ple_stage(i, last=(i == DEPTH - 1))
        S.emit()
    return nc


def make_consts(T):
    H_, C = RET_H, RET_C
    half = RET_DK // 2
    inv_freq = (10000.0 ** (-np.arange(half, dtype=np.float32) / half)).astype(np.float32)
    pos = np.arange(T, dtype=np.float32)
    ang = (pos[None, :] * inv_freq[:, None]).astype(np.float32)
    cos, sin = np.cos(ang).astype(np.float32), np.sin(ang).astype(np.float32)
    sc = np.float32(RET_DK ** -0.5)
    rope = np.stack([cos, sin]).astype(np.float32)
    log_gamma = np.log1p(-np.exp2(-5.0 - np.arange(H_, dtype=np.float32))).astype(np.float32)
    idx = np.arange(C, dtype=np.float32)
    diff = idx[:, None] - idx[None, :]
    decay = np.where(diff[None] >= 0, np.exp(log_gamma[:, None, None] * np.maximum(diff, 0.0)[None]), 0.0)
    dmask = np.ascontiguousarray(decay.transpose(2, 0, 1)).astype(np.float32)
    xi = np.exp(log_gamma[:, None] * (idx + 1)[None, :]).astype(np.float32)
    xi_bc = np.ascontiguousarray(np.broadcast_to(np.repeat(xi, 2, axis=0)[None], (128, 2 * H_, C))).astype(np.float32)
    zeta = np.exp(log_gamma[:, None] * (C - 1 - idx)[None, :]).astype(np.float32).T.copy()
    zeta = np.ascontiguousarray(np.repeat(zeta, RET_DK, axis=1)).astype(np.float32)
    kk = np.arange(128)
    maskneg = np.where(kk[:, None] <= kk[None, :], 0.0, NEG).astype(np.float32)
    return {
        "c_ident": np.eye(128, dtype=np.float32),
        "c_rope": rope,
        "c_dmask": dmask,
        "c_xi": xi_bc,
        "c_zeta": zeta,
        "c_maskneg": maskneg,
    }


def make_in_maps(inputs, T, nb):
    f = lambda a: np.ascontiguousarray(np.asarray(a, dtype=np.float32))
    consts = make_consts(T)
    shared = {
        "norm_w": f(inputs["norm_w"]),
        "ffn_w_in": f(inputs["ffn_w_in"]),
        "ffn_w_out": f(inputs["ffn_w_out"]),
        "ret_w_in": f(inputs["ret_w_in"])[0],
        "ret_gn_w": f(inputs["ret_gn_w"])[0].reshape(-1),
        "ret_w_out": f(inputs["ret_w_out"])[0],
        "fox_w_in": f(inputs["fox_w_in"])[0],
        "fox_b_f": f(inputs["fox_b_f"])[0].reshape(FOX_H, 1),
        "fox_w_out": f(inputs["fox_w_out"])[0],
        "ple_w_proj": f(inputs["ple_w_proj"]),
        "ple_w_gate": f(inputs["ple_w_gate"]),
        "final_norm_w": f(inputs["final_norm_w"]),
    }
    shared.update(consts)
    x = f(inputs["x"])
    p = f(inputs["p"])
    maps = []
    for b in range(nb):
        m = dict(shared)
        m["x"] = np.ascontiguousarray(x[b])
        m["p"] = np.ascontiguousarray(p[:, b])
        maps.append(m)
    return maps


_NC_CACHE = {}


def kernel(**inputs):
    x = np.asarray(inputs["x"])
    nb, T = x.shape[0], x.shape[1]
    if T not in _NC_CACHE:
        _NC_CACHE[T] = build(T)
    nc = _NC_CACHE[T]
    in_maps = make_in_maps(inputs, T, nb)
    res = run_bass_kernel_spmd(nc, in_maps, core_ids=list(range(nb)))
    return np.stack([np.asarray(r["out"], dtype=np.float32) for r in res.results], axis=0)
```

```python
import numpy as np
from contextlib import ExitStack
import concourse.bass as bass
import concourse.mybir as mybir
from concourse.bass_utils import run_bass_kernel_spmd

F32 = mybir.dt.float32
BF16 = mybir.dt.bfloat16
AF = mybir.ActivationFunctionType
ALU = mybir.AluOpType

D = 1024
DFF = 2816
NFC = DFF // 128
DPLE = 256
DEPTH = 2
EPS = 1e-6
RET_H, RET_DK, RET_DV, RET_C = 4, 256, 512, 128
RET_IN = 6144
FOX_H, FOX_DH = 16, 64
FOX_IN = 3088
NEG = -30000.0
USE_POW = True

ENGS = ["pe", "act", "dve", "pool", "sp"]


class Buf:
    __slots__ = ("w", "rs", "sem")

    def __init__(self):
        self.w = None
        self.rs = []
        self.sem = {}


class DSem:
    __slots__ = ("sem", "count", "last")

    def __init__(self, sem):
        self.sem = sem
        self.count = 0
        self.last = None


class Op:
    __slots__ = ("eng", "fn", "deps", "sig", "cnt", "dsem", "dcnt", "epoch")

    def __init__(self, eng, fn, epoch):
        self.eng = eng
        self.fn = fn
        self.deps = []
        self.sig = False
        self.cnt = 0
        self.dsem = None
        self.dcnt = 0
        self.epoch = epoch


class Sched:
    def __init__(self, nc):
        self.nc = nc
        self.ops = {e: [] for e in ENGS}
        self.esem = {e: nc.alloc_semaphore(name="es_" + e) for e in ENGS}
        self.nsem = 0
        self.free = {e: [] for e in ENGS}
        self.live = []
        self.epoch = 0

    def _track(self, op, r, w):
        deps = []
        for b in r:
            if b.w is not None:
                deps.append(b.w)
        for b in w:
            if b.w is not None:
                deps.append(b.w)
            deps.extend(b.rs)
        for b in w:
            b.w = op
            b.rs = []
        for b in r:
            b.rs.append(op)
        seen = set()
        for d in deps:
            if d is op or id(d) in seen or d.epoch < self.epoch:
                continue
            if d.eng == "pe" and op.eng == "pe":
                continue
            seen.add(id(d))
            op.deps.append(d)
            if d.dsem is None:
                d.sig = True

    def op(self, eng, fn, r=(), w=()):
        o = Op(eng, fn, self.epoch)
        self._track(o, r, w)
        self.ops[eng].append(o)
        return o

    def dma(self, eng, out, in_, r=(), w=(), chan=None):
        if eng not in chan.sem:
            if self.free[eng]:
                ds = self.free[eng].pop()
            else:
                ds = DSem(self.nc.alloc_semaphore(name="ds_%d" % self.nsem))
                self.nsem += 1
            chan.sem[eng] = ds
            self.live.append((chan, eng, ds))
        ds = chan.sem[eng]
        ds.count += 16

        def fn(e, out=out, in_=in_):
            return e.dma_start(out=out, in_=in_)
        o = Op(eng, fn, self.epoch)
        o.dsem = ds.sem
        o.dcnt = ds.count
        ds.last = o
        self._track(o, r, w)
        self.ops[eng].append(o)
        return o

    def barrier(self):
        lasts = []
        for e in ENGS:
            for o in reversed(self.ops[e]):
                if o.dsem is None and o.fn is not None:
                    o.sig = True
                    lasts.append(o)
                    break
        dlast = [ds.last for (_, _, ds) in self.live if ds.last is not None]
        for e in ENGS:
            o = Op(e, None, self.epoch)
            o.deps = list(lasts) + dlast
            self.ops[e].append(o)
        for (b, e, ds) in self.live:
            del b.sem[e]
            self.free[e].append(ds)
        self.live = []
        self.epoch += 1

    def emit(self):
        nc = self.nc
        for e in ENGS:
            c = 0
            for o in self.ops[e]:
                if o.dsem is None and o.sig:
                    c += 1
                    o.cnt = c

        def run(ename, eng):
            waited = {}
            for o in self.ops[ename]:
                need = {}
                for d in o.deps:
                    if d.dsem is not None:
                        s, v = d.dsem, d.dcnt
                    else:
                        s, v = self.esem[d.eng], d.cnt
                    if s.num not in need or need[s.num][1] < v:
                        need[s.num] = (s, v)
                for num, (s, v) in need.items():
                    if waited.get(num, 0) >= v:
                        continue
                    waited[num] = v
                    eng.wait_ge(s, v)
                if o.fn is None:
                    continue
                ins = o.fn(eng)
                if o.dsem is not None:
                    ins.then_inc(o.dsem, 16)
                elif o.sig:
                    ins.then_inc(self.esem[ename], 1)

        with nc.Block() as block:
            @block.tensor
            def _(e):
                run("pe", e)

            @block.scalar
            def _(e):
                run("act", e)

            @block.vector
            def _(e):
                run("dve", e)

            @block.gpsimd
            def _(e):
                run("pool", e)

            @block.sync
            def _(e):
                run("sp", e)


def build(T):
    NT = T // 512
    NS = T // 128
    nc = bass.Bass("TRN2", target_bir_lowering=False)

    def din(name, shape):
        return nc.dram_tensor(name, list(shape), F32, kind="ExternalInput").ap()

    x_in = din("x", [T, D])
    p_in = din("p", [DEPTH, T, DPLE])
    norm_w = din("norm_w", [DEPTH, 4, D])
    ffn_w_in = din("ffn_w_in", [DEPTH, 2, D, 2 * DFF])
    ffn_w_out = din("ffn_w_out", [DEPTH, 2, DFF, D])
    ret_w_in = din("ret_w_in", [D, RET_IN])
    ret_gn_w = din("ret_gn_w", [RET_H * RET_DV])
    ret_w_out = din("ret_w_out", [RET_H * RET_DV, D])
    fox_w_in = din("fox_w_in", [D, FOX_IN])
    fox_b_f = din("fox_b_f", [FOX_H, 1])
    fox_w_out = din("fox_w_out", [D, D])
    ple_w_proj = din("ple_w_proj", [DEPTH, DPLE, D])
    ple_w_gate = din("ple_w_gate", [DEPTH, D, D])
    final_norm_w = din("final_norm_w", [D])
    c_ident = din("c_ident", [128, 128])
    c_rope = din("c_rope", [2, 128, T])
    c_dmask = din("c_dmask", [128, 4, 128])
    c_xi = din("c_xi", [128, 8, 128])
    c_zeta = din("c_zeta", [128, 1024])
    c_maskneg = din("c_maskneg", [128, 128])
    out = nc.dram_tensor("out", [T, D], F32, kind="ExternalOutput").ap()

    H = nc.dram_tensor("H", [T, D], F32).ap()
    QKs = nc.dram_tensor("QKs", [16 * 128, T], BF16).ap()
    VGs = nc.dram_tensor("VGs", [T, 4096], BF16).ap()
    QA = nc.dram_tensor("QA", [FOX_H, 70, T], BF16).ap()
    KA = nc.dram_tensor("KA", [FOX_H, 70, T], BF16).ap()
    Vs = nc.dram_tensor("Vs", [FOX_H, 128, NS, 64], BF16).ap()

    S = Sched(nc)
    uid = [0]

    def un(name):
        uid[0] += 1
        return "%s_%d" % (name, uid[0])
    gam = [1.0 - 2.0 ** (-5.0 - h) for h in range(RET_H)]
    gamma_c = [float(np.exp(np.log1p(-2.0 ** (-5.0 - h)) * RET_C)) for h in range(RET_H)]

    with ExitStack() as g:
        def gsb(name, shape, dt):
            return g.enter_context(nc.sbuf_tensor(un(name), list(shape), dt))

        PT = [g.enter_context(nc.psum_tensor("pt%d" % i, [128, 1024], BF16)) for i in range(2)]
        BPT = [Buf() for _ in range(2)]
        PS = [g.enter_context(nc.psum_tensor("ps%d" % i, [128, 512], F32)) for i in range(6)]
        BPS = [Buf() for _ in range(6)]
        ptc = [0]

        ident_f = gsb("ident_f", [128, 128], F32)
        ident = gsb("ident", [128, 128], BF16)
        B_identf, B_ident = Buf(), Buf()
        S.dma("sp", ident_f[:], c_ident, w=[B_identf], chan=B_identf)
        S.op("dve", lambda e: e.tensor_copy(out=ident[:], in_=ident_f[:]), r=[B_identf], w=[B_ident])

        def mm(out_, lhsT, rhs, start, stop, r, w, skip=False):
            S.op("pe", lambda e: e.matmul(out_, lhsT=lhsT, rhs=rhs, start=start, stop=stop,
                                          skip_group_check=skip), r=r, w=w)

        def tr(out_, in_, r, w):
            S.op("pe", lambda e: e.transpose(out=out_, in_=in_, identity=ident[:]), r=list(r) + [B_ident], w=w)

        def act(out_, in_, func, r, w, **kw):
            S.op("act", lambda e: e.activation(out=out_, in_=in_, func=func, **kw), r=r, w=w)

        def tt(eng, out_, in0, in1, op, r, w):
            S.op(eng, lambda e: e.tensor_tensor(out=out_, in0=in0, in1=in1, op=op), r=r, w=w)

        def ts(eng, out_, in0, s1, op0, r, w, s2=None, op1=None):
            if op1 is None:
                S.op(eng, lambda e: e.tensor_scalar(out=out_, in0=in0, scalar1=s1, scalar2=None, op0=op0), r=r, w=w)
            else:
                S.op(eng, lambda e: e.tensor_scalar(out=out_, in0=in0, scalar1=s1, scalar2=s2, op0=op0, op1=op1),
                     r=r, w=w)

        def stt(out_, in0, scalar, in1, op0, op1, r, w):
            S.op("dve", lambda e: e.scalar_tensor_tensor(out=out_, in0=in0, scalar=scalar, in1=in1, op0=op0, op1=op1),
                 r=r, w=w)

        def cp(eng, out_, in_, r, w):
            if eng == "act":
                act(out_, in_, AF.Copy, r, w)
            else:
                S.op(eng, lambda e: e.tensor_copy(out=out_, in_=in_), r=r, w=w)

        def recip(out_, in_, r, w):
            S.op("dve", lambda e: e.reciprocal(out=out_, in_=in_), r=r, w=w)

        def memset(eng, ap, val, w):
            S.op(eng, lambda e: e.memset(ap, val), w=w)

        hslot0 = gsb("hslot0", [128, 4, D], F32)
        pref = {"ready": False}
        D_H0 = Buf()
        mhalf = gsb("mhalf", [128, 4], F32)
        B_mhalf = Buf()
        memset("pool", mhalf[:], -0.5, [B_mhalf])

        def rstd_chain(ss, B_ss, tmp, B_tmp, rs, B_rs):
            ts("dve", tmp, ss, EPS, ALU.add, (B_ss if isinstance(B_ss, list) else [B_ss]), [B_tmp])
            if USE_POW:
                tt("pool", rs, tmp, mhalf[:], ALU.pow, [B_tmp, B_mhalf], [B_rs])
            else:
                act(tmp, tmp, AF.Sqrt, [B_tmp], [B_tmp])
                recip(rs, tmp, [B_tmp], [B_rs])

        Hv = H.rearrange("(t s p) d -> t p s d", s=4, p=128)
        Xv = x_in.rearrange("(t s p) d -> t p s d", s=4, p=128)
        Ov = out.rearrange("(t s p) d -> t p s d", s=4, p=128)

        class TileCtx:
            def __init__(self, es, nwsrc, src=None, nslots=2, nx=1):
                self.src = Hv if src is None else src
                sb = lambda name, shape, dt: es.enter_context(nc.sbuf_tensor(un(name), list(shape), dt))
                self.h = [hslot0] + [sb("h%d" % i, [128, 4, D], F32) for i in range(1, nslots)]
                self.Bh = [[Buf() for _ in range(4)] for _ in range(nslots)]
                self.nslots = nslots
                self.nwb = sb("nwb", [128, D], F32)
                self.B_nwb = Buf()
                S.dma("sp", self.nwb[:], nwsrc.partition_broadcast(128), w=[self.B_nwb], chan=self.B_nwb)
                self.xns = [sb("xn%d" % i, [128, D], BF16) for i in range(2)]
                self.B_xns = [Buf(), Buf()]
                self.nx = nx
                self.xnTs = [sb("xnT%d" % i, [128, 8, 512], BF16) for i in range(nx)]
                self.B_xnTs = [[Buf() for _ in range(8)] for _ in range(nx)]
                self.xnT, self.B_xnT = self.xnTs[0], self.B_xnTs[0]
                self.ss = sb("ss", [128, 4], F32)
                self.sst = sb("sst", [128, 4], F32)
                self.rs = sb("rs", [128, 4], F32)
                self.B_ss, self.B_sst, self.B_rs = [Buf() for _ in range(4)], Buf(), Buf()

            def use(self, t):
                self.xnT, self.B_xnT = self.xnTs[t % self.nx], self.B_xnTs[t % self.nx]

            def load(self, t):
                sl = t % self.nslots
                if t == 0 and pref["ready"]:
                    pref["ready"] = False
                    return
                S.dma("sp", self.h[sl][:], self.src[t], w=self.Bh[sl], chan=self.Bh[sl][0])

            def prefetch_next(self, t):
                if t == NT - 1 and (NT - 1) % self.nslots != 0:
                    S.dma("sp", hslot0[:], Hv[0], r=[D_H0], w=self.Bh[0], chan=self.Bh[0][0])
                    pref["ready"] = True

            def _stt(self, t, s):
                sl = t % self.nslots
                h, Bh = self.h[sl], self.Bh[sl]
                xn, B_xn = self.xns[s % 2], self.B_xns[s % 2]
                stt(xn[:], h[:, s, :], self.rs[:, s:s + 1], self.nwb[:], ALU.mult, ALU.mult,
                    [Bh[s], self.B_rs, self.B_nwb], [B_xn])

            def norm_stats(self, t):
                sl = t % self.nslots
                h, Bh = self.h[sl], self.Bh[sl]
                for s in range(4):
                    act(self.xns[s % 2][:], h[:, s, :], AF.Square, [Bh[s]], [self.B_xns[s % 2], self.B_ss[s]],
                        scale=1.0 / 32.0, accum_out=self.ss[:, s:s + 1])
                rstd_chain(self.ss[:], self.B_ss, self.sst[:], self.B_sst, self.rs[:], self.B_rs)
                self._stt(t, 0)

            def norm_piece(self, t, g):
                xnT, B_xnT = self.xnTs[t % self.nx], self.B_xnTs[t % self.nx]
                s, half = g // 2, g % 2
                xn, B_xn = self.xns[s % 2], self.B_xns[s % 2]
                if half == 1 and s < 3:
                    self._stt(t, s + 1)
                b = ptc[0] % 2
                ptc[0] += 1
                for k4 in range(4):
                    kc = half * 4 + k4
                    tr(PT[b][:, k4 * 128:(k4 + 1) * 128], xn[:, kc * 128:(kc + 1) * 128], [B_xn], [BPT[b]])
                eng = "act" if half == 0 else "dve"
                cp(eng, xnT[:, half * 4:(half + 1) * 4, s * 128:(s + 1) * 128],
                   PT[b][:, 0:512].rearrange("p (k t) -> p k t", k=4),
                   [BPT[b]], B_xnT[half * 4:(half + 1) * 4])

            def norm_T(self, t):
                self.norm_stats(t)
                for g in range(8):
                    self.norm_piece(t, g)

        def load_w_cols(es, name, src2d, K, N, blocks):
            kc_n = K // 128
            wt = es.enter_context(nc.sbuf_tensor(un(name), [128, kc_n, N], BF16))
            srcv = src2d.rearrange("(k p) n -> p k n", p=128)
            Bw = {}
            for (c0, c1) in blocks:
                b = Buf()
                for c in range(c0, c1, 128):
                    Bw[c] = b
                S.dma("pool", wt[:, :, c0:c1], srcv[:, :, c0:c1], w=[b], chan=b)
            return wt, (lambda col: Bw[(col // 128) * 128])

        def load_w(es, name, src2d, K, N, ncols_split=1):
            kc_n = K // 128
            wt = es.enter_context(nc.sbuf_tensor(un(name), [128, kc_n, N], BF16))
            Bw = [Buf() for _ in range(kc_n)]
            for kc in range(kc_n):
                S.dma("pool", wt[:, kc, :], src2d[kc * 128:(kc + 1) * 128, :], w=[Bw[kc]], chan=Bw[kc])
            return wt, Bw


        def ffn_stage(i, j, first):
            with ExitStack() as es:
                sb = lambda name, shape, dt: es.enter_context(nc.sbuf_tensor(un(name), list(shape), dt))
                blocks = []
                for c0 in range(0, NFC, 4):
                    c1 = min(c0 + 4, NFC)
                    blocks.append((c0 * 128, c1 * 128))
                    blocks.append((DFF + c0 * 128, DFF + c1 * 128))
                win, Bwin = load_w_cols(es, "win", ffn_w_in[i, j], D, 2 * DFF, blocks)
                wout, Bwout = load_w(es, "wout", ffn_w_out[i, j], DFF, D)
                tc = TileCtx(es, norm_w[i, 2 * j], src=Xv if first else None)
                hid = sb("hid", [128, NFC, 512], BF16)
                B_hid = [Buf() for _ in range(NFC)]
                sg = [sb("sg%d" % k, [128, 512], F32) for k in range(2)]
                B_sg = [Buf(), Buf()]
                tc.load(0)
                tc.norm_T(0)
                for t in range(NT):
                    if t + 1 < NT:
                        tc.load(t + 1)
                    tc.prefetch_next(t)
                    sl = t % 2
                    for c in range(NFC):
                        pg, pu = (0, 1) if c % 2 == 0 else (2, 3)
                        for kc in range(8):
                            mm(PS[pg][:], win[:, kc, c * 128:(c + 1) * 128], tc.xnT[:, kc, :], kc == 0, kc == 7,
                               [Bwin(c * 128), tc.B_xnT[kc]], [BPS[pg]])
                        for kc in range(8):
                            mm(PS[pu][:], win[:, kc, DFF + c * 128:DFF + (c + 1) * 128], tc.xnT[:, kc, :],
                               kc == 0, kc == 7, [Bwin(DFF + c * 128), tc.B_xnT[kc]], [BPS[pu]])
                        k = c % 2
                        act(sg[k][:], PS[pg][:], AF.Silu, [BPS[pg]], [B_sg[k]])
                        tt("dve", hid[:, c, :], sg[k][:], PS[pu][:], ALU.mult, [B_sg[k], BPS[pu]], [B_hid[c]])
                        if c == NFC // 2 and t + 1 < NT:
                            tc.norm_stats(t + 1)
                    q = 0
                    for s in range(4):
                        for half in range(2):
                            if t + 1 < NT:
                                tc.norm_piece(t + 1, q)
                            b = 4 + (q % 2)
                            q += 1
                            for c in range(NFC):
                                mm(PS[b][:], hid[:, c, s * 128:(s + 1) * 128], wout[:, c, half * 512:(half + 1) * 512],
                                   c == 0, c == NFC - 1, [B_hid[c], Bwout[c]], [BPS[b]])
                            hv = tc.h[sl][:, s, half * 512:(half + 1) * 512]
                            stt(hv, PS[b][:], 0.5, hv, ALU.mult, ALU.add, [BPS[b], tc.Bh[sl][s]], [tc.Bh[sl][s]])
                    S.dma("sp", Hv[t], tc.h[sl][:], r=tc.Bh[sl], w=([D_H0] if t == 0 else []), chan=tc.Bh[sl][0])
            S.barrier()

        def ple_stage(i, last):
            with ExitStack() as es:
                sb = lambda name, shape, dt: es.enter_context(nc.sbuf_tensor(un(name), list(shape), dt))
                wg, Bwg = load_w(es, "wg", ple_w_gate[i], D, D)
                wp, Bwp = load_w(es, "wp", ple_w_proj[i], DPLE, D)
                tc = TileCtx(es, norm_w[i, 3], nslots=3, nx=2)
                pv = p_in[i].rearrange("(t s p) d -> t p s d", s=4, p=128)
                pb = [sb("pb%d" % k, [128, 4, DPLE], BF16) for k in range(2)]
                B_pb = [Buf(), Buf()]
                pT = sb("pT", [128, 2, 512], BF16)
                B_pT = Buf()
                sgt = [sb("sgt%d" % k, [128, 512], F32) for k in range(4)]
                B_sgt = [Buf() for _ in range(4)]
                if last:
                    fnw = sb("fnw", [128, D], F32)
                    B_fnw = Buf()
                    S.dma("sp", fnw[:], final_norm_w.partition_broadcast(128), w=[B_fnw], chan=B_fnw)
                    junk = sb("junk", [128, D], BF16)
                    B_junk = Buf()
                    fss = sb("fss", [128, 4], F32)
                    fst = sb("fst", [128, 4], F32)
                    frs = sb("frs", [128, 4], F32)
                    B_fss, B_fst, B_frs = Buf(), Buf(), Buf()
                tc.load(0)
                S.dma("pool", pb[0][:], pv[0], w=[B_pb[0]], chan=B_pb[0])
                if NT > 1:
                    tc.load(1)
                tc.norm_T(0)
                if NT > 1:
                    tc.norm_stats(1)
                for t in range(NT):
                    if t + 2 < NT:
                        tc.load(t + 2)
                    if t + 1 < NT:
                        S.dma("pool", pb[(t + 1) % 2][:], pv[t + 1], w=[B_pb[(t + 1) % 2]], chan=B_pb[(t + 1) % 2])
                    if not last:
                        tc.prefetch_next(t)
                    sl = t % 3
                    psl = t % 2
                    tc.use(t)
                    for k2 in range(2):
                        b = ptc[0] % 2
                        ptc[0] += 1
                        for s in range(4):
                            tr(PT[b][:, s * 128:(s + 1) * 128], pb[psl][:, s, k2 * 128:(k2 + 1) * 128],
                               [B_pb[psl]], [BPT[b]])
                        cp("act", pT[:, k2, :], PT[b][:, 0:512], [BPT[b]], [B_pT])
                    q = 0
                    for s in range(4):
                        for half in range(2):
                            if t + 1 < NT:
                                tc.norm_piece(t + 1, q)
                            bg, bp = ((0, 1), (2, 3), (4, 5))[q % 3]
                            k = q % 4
                            q += 1
                            for kc in range(8):
                                mm(PS[bg][:], tc.xnT[:, kc, s * 128:(s + 1) * 128], wg[:, kc, half * 512:(half + 1) * 512],
                                   kc == 0, kc == 7, [tc.B_xnT[kc], Bwg[kc]], [BPS[bg]])
                            for kc in range(2):
                                mm(PS[bp][:], pT[:, kc, s * 128:(s + 1) * 128], wp[:, kc, half * 512:(half + 1) * 512],
                                   kc == 0, kc == 1, [B_pT, Bwp[kc]], [BPS[bp]])
                            act(sgt[k][:], PS[bg][:], AF.Sigmoid, [BPS[bg]], [B_sgt[k]])
                            tt("dve", sgt[k][:], sgt[k][:], PS[bp][:], ALU.mult, [B_sgt[k], BPS[bp]], [B_sgt[k]])
                            hv = tc.h[sl][:, s, half * 512:(half + 1) * 512]
                            tt("pool" if half == 0 else "dve", hv, hv, sgt[k][:], ALU.add, [B_sgt[k], tc.Bh[sl][s]],
                               [tc.Bh[sl][s]])
                    if t + 2 < NT:
                        tc.norm_stats(t + 2)
                    if last:
                        for s in range(4):
                            act(junk[:], tc.h[sl][:, s, :], AF.Square, [tc.Bh[sl][s]], [B_junk, B_fss],
                                scale=1.0 / 32.0, accum_out=fss[:, s:s + 1])
                        rstd_chain(fss[:], B_fss, fst[:], B_fst, frs[:], B_frs)
                        for s in range(4):
                            stt(tc.h[sl][:, s, :], tc.h[sl][:, s, :], frs[:, s:s + 1], fnw[:], ALU.mult, ALU.mult,
                                [tc.Bh[sl][s], B_frs, B_fnw], [tc.Bh[sl][s]])
                        S.dma("sp", Ov[t], tc.h[sl][:], r=tc.Bh[sl], chan=tc.Bh[sl][0])
                    else:
                        S.dma("sp", Hv[t], tc.h[sl][:], r=tc.Bh[sl], w=([D_H0] if t == 0 else []), chan=tc.Bh[sl][0])
            S.barrier()

        def ret_stage1(i):
            with ExitStack() as es:
                sb = lambda name, shape, dt: es.enter_context(nc.sbuf_tensor(un(name), list(shape), dt))
                win, Bwin = load_w_cols(es, "rwin", ret_w_in, D, RET_IN,
                                        [(c, c + 512) for c in range(0, RET_IN, 512)])
                tc = TileCtx(es, norm_w[i, 1], nx=2)
                rope = [sb("rope%d" % k, [128, 2, 512], F32) for k in range(2)]
                B_rope = [Buf(), Buf()]
                ropev = c_rope.rearrange("f p t -> p f t")
                qk = sb("qk", [128, 16, 512], BF16)
                B_qk = [Buf() for _ in range(16)]
                tmp = [sb("rt%d" % k, [128, 512], F32) for k in range(4)]
                B_tmp = [Buf() for _ in range(4)]
                vg = [sb("vg%d" % k, [128, 4096], BF16) for k in range(2)]
                B_vg = [Buf(), Buf()]
                QKv = QKs.rearrange("(c p) t -> p c t", p=128)
                tc.load(0)
                tc.norm_T(0)
                S.dma("sp", rope[0][:], ropev[:, :, 0:512], w=[B_rope[0]], chan=B_rope[0])
                nvg = 0
                for t in range(NT):
                    if t + 1 < NT:
                        tc.load(t + 1)
                        S.dma("sp", rope[(t + 1) % 2][:], ropev[:, :, (t + 1) * 512:(t + 2) * 512],
                              w=[B_rope[(t + 1) % 2]], chan=B_rope[(t + 1) % 2])
                    sl = t % 2
                    tc.use(t)
                    for pr in range(8):
                        if pr == 2 and t + 1 < NT:
                            tc.norm_stats(t + 1)
                        isk = pr >= 4
                        oc1, oc2 = 2 * pr, 2 * pr + 1
                        b1, b2 = (0, 1) if pr % 2 == 0 else (2, 3)
                        for (oc, b) in ((oc1, b1), (oc2, b2)):
                            for kc in range(8):
                                mm(PS[b][:], win[:, kc, oc * 128:(oc + 1) * 128], tc.xnT[:, kc, :], kc == 0, kc == 7,
                                   [Bwin(oc * 128), tc.B_xnT[kc]], [BPS[b]])
                        cos = rope[sl][:, 0, :]
                        sin = rope[sl][:, 1, :]

                        def rmul(o_, p_, tab, r_, w_, isk=isk):
                            if isk:
                                stt(o_, p_, float(RET_DK ** -0.5), tab, ALU.mult, ALU.mult, r_, w_)
                            else:
                                tt("dve", o_, p_, tab, ALU.mult, r_, w_)
                        rmul(tmp[0][:], PS[b1][:], cos, [BPS[b1], B_rope[sl]], [B_tmp[0]])
                        rmul(tmp[1][:], PS[b2][:], sin, [BPS[b2], B_rope[sl]], [B_tmp[1]])
                        tt("pool", qk[:, oc1, :], tmp[0][:], tmp[1][:], ALU.subtract, [B_tmp[0], B_tmp[1]], [B_qk[oc1]])
                        rmul(tmp[2][:], PS[b1][:], sin, [BPS[b1], B_rope[sl]], [B_tmp[2]])
                        rmul(tmp[3][:], PS[b2][:], cos, [BPS[b2], B_rope[sl]], [B_tmp[3]])
                        tt("pool", qk[:, oc2, :], tmp[2][:], tmp[3][:], ALU.add, [B_tmp[2], B_tmp[3]], [B_qk[oc2]])
                    S.dma("sp", QKv[:, :, t * 512:(t + 1) * 512], qk[:], r=B_qk, chan=B_qk[0])
                    q = 0
                    for s in range(4):
                        k = nvg % 2
                        nvg += 1
                        for n in range(8):
                            if n % 4 == 0 and t + 1 < NT:
                                tc.norm_piece(t + 1, 2 * s + n // 4)
                            b = 4 + (q % 2)
                            q += 1
                            for kc in range(8):
                                mm(PS[b][:], tc.xnT[:, kc, s * 128:(s + 1) * 128],
                                   win[:, kc, 2048 + n * 512:2048 + (n + 1) * 512], kc == 0, kc == 7,
                                   [tc.B_xnT[kc], Bwin(2048 + n * 512)], [BPS[b]])
                            act(vg[k][:, n * 512:(n + 1) * 512], PS[b][:], AF.Copy if n < 4 else AF.Silu,
                                [BPS[b]], [B_vg[k]])
                        r0 = t * 512 + s * 128
                        S.dma("sp", VGs[r0:r0 + 128, :], vg[k][:], r=[B_vg[k]], chan=B_vg[k])
            S.barrier()

        def ret_stage2(i):
            with ExitStack() as es:
                sb = lambda name, shape, dt: es.enter_context(nc.sbuf_tensor(un(name), list(shape), dt))
                wout, Bwout = load_w(es, "rwout", ret_w_out, RET_H * RET_DV, D)
                dmask = sb("dmask", [128, 4, 128], F32)
                xi = sb("xi", [128, 8, 128], F32)
                zeta = sb("zeta", [128, 1024], F32)
                gnw = sb("gnw", [128, 2048], F32)
                B_c = Buf()
                S.dma("sp", dmask[:], c_dmask, w=[B_c], chan=B_c)
                B_c2 = Buf()
                S.dma("sp", xi[:], c_xi, w=[B_c2], chan=B_c2)
                B_c3 = Buf()
                S.dma("sp", zeta[:], c_zeta, w=[B_c3], chan=B_c3)
                B_c4 = Buf()
                S.dma("sp", gnw[:], ret_gn_w.partition_broadcast(128), w=[B_c4], chan=B_c4)
                Sf = sb("Sf", [128, 8, 512], F32)
                Sb_ = sb("Sb", [128, 8, 512], BF16)
                B_Sf = [Buf() for _ in range(8)]
                B_Sb = [Buf() for _ in range(8)]
                for k in range(8):
                    memset("pool", Sf[:, k, :], 0.0, [B_Sf[k]])
                    memset("pool", Sb_[:, k, :], 0.0, [B_Sb[k]])
                NQ = 3
                qk = [sb("qkc%d" % k, [128, 16, 128], BF16) for k in range(NQ)]
                B_qk = [Buf() for _ in range(NQ)]
                vg = [sb("vgc%d" % k, [128, 4096], BF16) for k in range(NQ)]
                B_vg = [Buf() for _ in range(NQ)]
                NHC = 4
                hc = [sb("hc%d" % k, [128, D], F32) for k in range(NHC)]
                B_hc = [Buf() for _ in range(NHC)]
                sTm = [sb("sTm%d" % k, [128, 4, 128], BF16) for k in range(2)]
                B_sTm = [Buf(), Buf()]
                qx = [sb("qx%d" % k, [128, 8, 128], BF16) for k in range(2)]
                B_qx = [Buf(), Buf()]
                kz = [sb("kz%d" % k, [128, 1024], BF16) for k in range(2)]
                B_kz = [Buf(), Buf()]
                ot = [sb("ot%d" % k, [128, 512], F32) for k in range(2)]
                B_ot = [Buf(), Buf()]
                ys = [sb("y%d" % k, [128, 2048], BF16) for k in range(2)]
                B_ys = [[Buf() for _ in range(4)] for _ in range(2)]
                yT = sb("yT", [128, 16, 128], BF16)
                B_yT = [Buf() for _ in range(4)]
                junk = sb("rjunk", [128, 512], BF16)
                B_junk = Buf()
                gss = sb("gss", [128, 4], F32)
                gst = sb("gst", [128, 4], F32)
                grs = sb("grs", [128, 4], F32)
                B_gss, B_gst, B_grs = Buf(), Buf(), Buf()
                QKv = QKs.rearrange("(c p) t -> p c t", p=128)
                Hc = H.rearrange("(n p) d -> n p d", p=128)

                def load(c):
                    k = c % NQ
                    S.dma("sp", qk[k][:], QKv[:, :, c * 128:(c + 1) * 128], w=[B_qk[k]], chan=B_qk[k])
                    S.dma("sp", vg[k][:], VGs[c * 128:(c + 1) * 128, :], w=[B_vg[k]], chan=B_vg[k])
                    S.dma("sp", hc[c % NHC][:], Hc[c], w=[B_hc[c % NHC]], chan=B_hc[c % NHC])

                def phaseA(c):
                    k = c % 2
                    kq = c % NQ
                    for h in range(4):
                        qA, qB = qk[kq][:, 2 * h, :], qk[kq][:, 2 * h + 1, :]
                        kA, kB = qk[kq][:, 8 + 2 * h, :], qk[kq][:, 8 + 2 * h + 1, :]
                        mm(PS[4][:, h * 128:(h + 1) * 128], kA, qA, True, False, [B_qk[kq]], [BPS[4]])
                        mm(PS[4][:, h * 128:(h + 1) * 128], kB, qB, False, True, [B_qk[kq]], [BPS[4]])
                    b = ptc[0] % 2
                    ptc[0] += 1
                    for kc in range(8):
                        tr(PT[b][:, kc * 128:(kc + 1) * 128], qk[kq][:, 8 + kc, :], [B_qk[kq]], [BPT[b]])
                    tt("dve", sTm[k][:], PS[4][:].rearrange("p (h t) -> p h t", h=4), dmask[:], ALU.mult,
                       [BPS[4], B_c], [B_sTm[k]])
                    tt("dve", kz[k][:], PT[b][:], zeta[:], ALU.mult, [BPT[b], B_c3], [B_kz[k]])
                    tt("pool", qx[k][:], qk[kq][:, 0:8, :], xi[:], ALU.mult, [B_qk[kq], B_c2], [B_qx[k]])

                def phaseC(c):
                    k = c % 2
                    kq = c % NQ
                    n = 0
                    for h in range(4):
                        vh = vg[kq][:, h * 512:(h + 1) * 512]
                        mm(PS[h][:], sTm[k][:, h, :], vh, True, False, [B_sTm[k], B_vg[kq]], [BPS[h]])
                        mm(PS[h][:], qx[k][:, 2 * h, :], Sb_[:, 2 * h, :], False, False, [B_qx[k], B_Sb[2 * h]], [BPS[h]])
                        mm(PS[h][:], qx[k][:, 2 * h + 1, :], Sb_[:, 2 * h + 1, :], False, True,
                           [B_qx[k], B_Sb[2 * h + 1]], [BPS[h]])
                        for dc in range(2):
                            b = 4 + (n % 2)
                            n += 1
                            sidx = 2 * h + dc
                            mm(PS[b][:], kz[k][:, sidx * 128:(sidx + 1) * 128], vh, True, True,
                               [B_kz[k], B_vg[kq]], [BPS[b]])
                            stt(Sf[:, sidx, :], Sf[:, sidx, :], gamma_c[h], PS[b][:], ALU.mult, ALU.add,
                                [BPS[b], B_Sf[sidx]], [B_Sf[sidx]])
                            cp("act" if dc == 0 else "pool", Sb_[:, sidx, :], Sf[:, sidx, :], [B_Sf[sidx]], [B_Sb[sidx]])

                def phaseD(c):
                    k = c % 2
                    kq = c % NQ
                    y, B_y = ys[c % 2], B_ys[c % 2]
                    for h in range(4):
                        act(junk[:], PS[h][:], AF.Square, [BPS[h]], [B_junk, B_gss],
                            scale=float(1.0 / np.sqrt(512.0)), accum_out=gss[:, h:h + 1])
                    rstd_chain(gss[:], B_gss, gst[:], B_gst, grs[:], B_grs)
                    for h in range(4):
                        o2 = h % 2
                        stt(ot[o2][:], PS[h][:], grs[:, h:h + 1], gnw[:, h * 512:(h + 1) * 512], ALU.mult, ALU.mult,
                            [BPS[h], B_grs, B_c4], [B_ot[o2]])
                        tt("pool", y[:, h * 512:(h + 1) * 512], ot[o2][:],
                           vg[kq][:, 2048 + h * 512:2048 + (h + 1) * 512],
                           ALU.mult, [B_ot[o2], B_vg[kq]], [B_y[h]])

                def phaseB(c):
                    k3 = c % NHC
                    y, B_y = ys[c % 2], B_ys[c % 2]
                    for gq in range(4):
                        b = ptc[0] % 2
                        ptc[0] += 1
                        for k4 in range(4):
                            kc = gq * 4 + k4
                            tr(PT[b][:, k4 * 128:(k4 + 1) * 128], y[:, kc * 128:(kc + 1) * 128], [B_y[gq]], [BPT[b]])
                        cp("act", yT[:, gq * 4:(gq + 1) * 4, :], PT[b][:, 0:512].rearrange("p (k t) -> p k t", k=4),
                           [BPT[b]], [B_yT[gq]])
                    for half in range(2):
                        b = 4 + half
                        for kc in range(16):
                            mm(PS[b][:], yT[:, kc, :], wout[:, kc, half * 512:(half + 1) * 512], kc == 0, kc == 15,
                               [B_yT[kc // 4], Bwout[kc]], [BPS[b]])
                        hv = hc[k3][:, half * 512:(half + 1) * 512]
                        tt("dve", hv, hv, PS[b][:], ALU.add, [BPS[b], B_hc[k3]], [B_hc[k3]])
                    S.dma("sp", Hc[c], hc[k3][:], r=[B_hc[k3]], w=([D_H0] if c < 4 else []), chan=B_hc[k3])

                load(0)
                if NS > 1:
                    load(1)
                phaseA(0)
                for c in range(NS):
                    phaseC(c)
                    if c + 2 < NS:
                        load(c + 2)
                    if c + 1 < NS:
                        phaseA(c + 1)
                    phaseD(c)
                    if c > 0:
                        phaseB(c - 1)
                    if c == NS - 1 and NS > 5:
                        B_p0 = Buf()
                        S.dma("sp", hslot0[:], Hv[0], r=[D_H0], w=[B_p0], chan=B_p0)
                        pref["ready"] = True
                phaseB(NS - 1)
            S.barrier()

        def fox_stage1(i):
            with ExitStack() as es:
                sb = lambda name, shape, dt: es.enter_context(nc.sbuf_tensor(un(name), list(shape), dt))
                win, Bwin = load_w_cols(es, "fwin", fox_w_in, D, FOX_IN,
                                        [(c, c + 512) for c in range(0, 3072, 512)] + [(3072, 3088)])
                tc = TileCtx(es, norm_w[i, 1], nx=2)
                negb = sb("negb", [16, 1], F32)
                B_negb = Buf()
                S.dma("sp", negb[:], fox_b_f, w=[B_negb], chan=B_negb)
                ts("dve", negb[:], negb[:], -1.0, ALU.mult, [B_negb], [B_negb])
                qs = [sb("qs%d" % k, [128, 8, 512], BF16) for k in range(2)]
                B_qs = [Buf(), Buf()]
                vs = [sb("vs%d" % k, [128, 1024], BF16) for k in range(2)]
                B_vs = [Buf(), Buf()]
                ef = sb("ef", [16, 512], F32)
                B_ef = Buf()
                lf = [sb("lf%d" % k, [16, 512], F32) for k in range(2)]
                B_lf = [Buf(), Buf()]
                ones = sb("ones", [16, 512], F32)
                CS = sb("CS", [16, T], F32)
                r1 = sb("r1", [16, 512], F32)
                P3 = sb("P3", [16, 3, 512], BF16)
                QC = [sb("QC%d" % k, [16, 3, 512], BF16) for k in range(2)]
                KC = [sb("KC%d" % k, [16, 3, 512], BF16) for k in range(2)]
                C1 = sb("C1", [16, 3, 512], BF16)
                C8 = sb("C8", [16, 3, 512], BF16)
                B_ones, B_CS, B_r1, B_P3, B_C1, B_C8 = Buf(), Buf(), Buf(), Buf(), Buf(), Buf()
                B_QC, B_KC = [Buf(), Buf()], [Buf(), Buf()]
                memset("pool", ones[:], 1.0, [B_ones])
                memset("pool", C1[:], 0.125, [B_C1])
                memset("pool", C8[:], 8.0, [B_C8])
                QAv = QA.rearrange("(m two) r t -> two r m t", two=2)
                KAv = KA.rearrange("(m two) r t -> two r m t", two=2)
                Vv = Vs.rearrange("h p n d -> p n h d")
                tc.load(0)
                tc.norm_T(0)
                nv = 0
                pending = []
                for t in range(NT):
                    if t + 1 < NT:
                        tc.load(t + 1)
                    tc.use(t)
                    q = 0
                    for which in range(2):
                        for m in range(8):
                            if which == 0 and m == 2 and t + 1 < NT:
                                tc.norm_stats(t + 1)
                            b = q % 4
                            q += 1
                            oc = which * 8 + m
                            for kc in range(8):
                                mm(PS[b][:], win[:, kc, oc * 128:(oc + 1) * 128], tc.xnT[:, kc, :], kc == 0, kc == 7,
                                   [Bwin(oc * 128), tc.B_xnT[kc]], [BPS[b]])
                            if m % 2 == 0:
                                act(qs[which][:, m, :], PS[b][:], AF.Copy, [BPS[b]], [B_qs[which]],
                                    scale=(0.125 if which == 0 else 1.0))
                            else:
                                ts("dve", qs[which][:, m, :], PS[b][:], (0.125 if which == 0 else 1.0), ALU.mult,
                                   [BPS[b]], [B_qs[which]])
                            if pending:
                                pending.pop(0)()
                        dst = QAv if which == 0 else KAv
                        for two in range(2):
                            S.dma("sp", dst[two][0:64, :, t * 512:(t + 1) * 512], qs[which][two * 64:(two + 1) * 64, :, :],
                                  r=[B_qs[which]], chan=B_qs[which])
                    for s in range(4):
                        k = nv % 2
                        nv += 1
                        for half in range(2):
                            if t + 1 < NT:
                                tc.norm_piece(t + 1, 2 * s + half)
                            b = 4 + half
                            for kc in range(8):
                                mm(PS[b][:], tc.xnT[:, kc, s * 128:(s + 1) * 128],
                                   win[:, kc, 2048 + half * 512:2048 + (half + 1) * 512], kc == 0, kc == 7,
                                   [tc.B_xnT[kc], Bwin(2048 + half * 512)], [BPS[b]])
                            cp("act" if half == 0 else "dve", vs[k][:, half * 512:(half + 1) * 512], PS[b][:],
                               [BPS[b]], [B_vs[k]])
                        S.dma("sp", Vv[:, t * 4 + s, :, :], vs[k][:].rearrange("p (h d) -> p h d", h=16),
                              r=[B_vs[k]], chan=B_vs[k])
                    for kc in range(8):
                        mm(PS[0][0:16, :], win[:, kc, 3072:3088], tc.xnT[:, kc, :], kc == 0, kc == 7,
                           [Bwin(3072), tc.B_xnT[kc]], [BPS[0]])
                    act(ef[:], PS[0][0:16, :], AF.Exp, [BPS[0], B_negb], [B_ef], scale=-1.0, bias=negb[:])
                    k2 = t % 2
                    act(lf[k2][:], ef[:], AF.Ln, [B_ef], [B_lf[k2]], bias=1.0)
                    cst = CS[:, t * 512:(t + 1) * 512]
                    init = 0.0 if t == 0 else CS[:, t * 512 - 1:t * 512]
                    S.op("dve", lambda e, cst=cst, init=init, k2=k2: e.tensor_tensor_scan(
                        out=cst, data0=ones[:], data1=lf[k2][:], initial=init, op0=ALU.mult, op1=ALU.add),
                        r=[B_ones, B_lf[k2], B_CS], w=[B_CS])
                    tsl = slice(t * 512, (t + 1) * 512)

                    def chain(cst=cst, k2=k2, tsl=tsl):
                        return [
                            lambda: cp("dve", P3[:, 0, :], cst, [B_CS], [B_P3]),
                            lambda: tt("dve", r1[:], cst, P3[:, 0, :], ALU.subtract, [B_CS, B_P3], [B_r1]),
                            lambda: cp("dve", P3[:, 1, :], r1[:], [B_r1], [B_P3]),
                            lambda: tt("dve", r1[:], r1[:], P3[:, 1, :], ALU.subtract, [B_r1, B_P3], [B_r1]),
                            lambda: cp("dve", P3[:, 2, :], r1[:], [B_r1], [B_P3]),
                            lambda: ts("dve", QC[k2][:], P3[:], -0.125, ALU.mult, [B_P3], [B_QC[k2]]),
                            lambda: ts("dve", KC[k2][:], P3[:], 8.0, ALU.mult, [B_P3], [B_KC[k2]]),
                            lambda: (S.dma("sp", QA[:, 64:67, tsl], QC[k2][:], r=[B_QC[k2]], chan=B_QC[k2]),
                                     S.dma("sp", KA[:, 67:70, tsl], KC[k2][:], r=[B_KC[k2]], chan=B_KC[k2]),
                                     S.dma("sp", QA[:, 67:70, tsl], C1[:], r=[B_C1], chan=B_C1),
                                     S.dma("sp", KA[:, 64:67, tsl], C8[:], r=[B_C8], chan=B_C8)),
                        ]
                    pending.extend(chain())
                while pending:
                    pending.pop(0)()
            S.barrier()

        def fox_stage3(O, B_O):
            with ExitStack() as es:
                sb = lambda name, shape, dt: es.enter_context(nc.sbuf_tensor(un(name), list(shape), dt))
                mneg_f = sb("mneg_f", [128, 128], F32)
                mneg = sb("mneg", [128, 128], BF16)
                B_mf, B_m = Buf(), Buf()
                S.dma("sp", mneg_f[:], c_maskneg, w=[B_mf], chan=B_mf)
                cp("dve", mneg[:], mneg_f[:], [B_mf], [B_m])
                qa = [sb("qa%d" % k, [128, T], BF16) for k in range(2)]
                ka = [sb("ka%d" % k, [128, T], BF16) for k in range(2)]
                va = [sb("va%d" % k, [128, NS, 65], BF16) for k in range(2)]
                B_qa, B_ka, B_va = [Buf(), Buf()], [Buf(), Buf()], [Buf(), Buf()]
                for k in range(2):
                    memset("pool", va[k][:, :, 64:65], 1.0, [B_va[k]])
                NPT = 4
                pt = [sb("ptx%d" % k, [128, 512], BF16) for k in range(NPT)]
                B_pt = [Buf() for _ in range(NPT)]
                rl = sb("rl", [128, 4], F32)
                B_rl = Buf()

                def load(h):
                    k = h % 2
                    S.dma("sp", qa[k][0:70, :], QA[h], w=[B_qa[k]], chan=B_qa[k])
                    S.dma("sp", ka[k][0:70, :], KA[h], w=[B_ka[k]], chan=B_ka[k])
                    S.dma("sp", va[k][:, :, 0:64], Vs[h], w=[B_va[k]], chan=B_va[k])
                load(0)
                B_p0 = Buf()
                S.dma("sp", hslot0[:], Hv[0], w=[B_p0], chan=B_p0)
                pref["ready"] = True
                items = [(h, iq, j) for h in range(FOX_H) for iq in range(NT) for j in range(4 * iq + 4)]
                LAG = 2

                def score(idx):
                    h, iq, j = items[idx]
                    k = h % 2
                    sb_ = idx % NPT
                    jj = j - 4 * iq
                    c0 = max(jj, 0) * 128
                    mm(PS[sb_][:, c0:512], ka[k][0:70, j * 128:(j + 1) * 128],
                       qa[k][0:70, iq * 512 + c0:(iq + 1) * 512], True, True,
                       [B_ka[k], B_qa[k]], [BPS[sb_]])
                    if jj >= 0:
                        mm(PS[sb_][:, c0:c0 + 128], ident[:], mneg[:], False, True, [B_ident, B_m], [BPS[sb_]],
                           skip=True)
                    act(pt[sb_][:, c0:512], PS[sb_][:, c0:512], AF.Exp, [BPS[sb_]], [B_pt[sb_]])

                def pv(idx):
                    h, iq, j = items[idx]
                    k = h % 2
                    sb_ = idx % NPT
                    jj = j - 4 * iq
                    ob = 4 + ((h * NT + iq) % 2)
                    if iq == 0 and j == 0 and h + 1 < FOX_H:
                        load(h + 1)
                    for qs_ in range(max(jj, 0), 4):
                        last = (j == 4 * iq + qs_)
                        mm(PS[ob][:, qs_ * 65:(qs_ + 1) * 65], pt[sb_][:, qs_ * 128:(qs_ + 1) * 128],
                           va[k][:, j, :], (j == 0 and qs_ == 0), last, [B_pt[sb_], B_va[k]], [BPS[ob]],
                           skip=True)
                    if j == 4 * iq + 3:
                        ov = PS[ob][:, 0:260].rearrange("p (a b) -> p a b", b=65)
                        recip(rl[:], ov[:, :, 64], [BPS[ob]], [B_rl])
                        for qs_ in range(4):
                            ts("dve", O[:, iq * 4 + qs_, h * 64:(h + 1) * 64], PS[ob][:, qs_ * 65:qs_ * 65 + 64],
                               rl[:, qs_:qs_ + 1], ALU.mult, [BPS[ob], B_rl], [B_O])

                for idx in range(len(items) + LAG):
                    if idx < len(items):
                        score(idx)
                    if idx >= LAG:
                        pv(idx - LAG)
            S.barrier()

        def fox_stage4(i, O, B_O):
            with ExitStack() as es:
                sb = lambda name, shape, dt: es.enter_context(nc.sbuf_tensor(un(name), list(shape), dt))
                wo, Bwo = load_w(es, "fwo", fox_w_out, D, D)
                h = [hslot0, sb("fh1", [128, 4, D], F32)]
                Bh = [Buf(), Buf()]
                oT = sb("oT", [128, 8, 128], BF16)
                B_oT = [Buf(), Buf()]
                if pref["ready"]:
                    pref["ready"] = False
                else:
                    S.dma("sp", h[0][:], Hv[0], w=[Bh[0]], chan=Bh[0])
                q = 0
                for t in range(NT):
                    if t + 1 < NT:
                        S.dma("sp", h[(t + 1) % 2][:], Hv[t + 1], w=[Bh[(t + 1) % 2]], chan=Bh[(t + 1) % 2])
                    sl = t % 2
                    if t == NT - 1 and (NT - 1) % 2 != 0:
                        S.dma("sp", hslot0[:], Hv[0], r=[D_H0], w=[Bh[0]], chan=Bh[0])
                        pref["ready"] = True
                    for s in range(4):
                        n = t * 4 + s
                        for half in range(2):
                            b = ptc[0] % 2
                            ptc[0] += 1
                            for k4 in range(4):
                                kc = half * 4 + k4
                                tr(PT[b][:, k4 * 128:(k4 + 1) * 128], O[:, n, kc * 128:(kc + 1) * 128], [B_O], [BPT[b]])
                            cp("act", oT[:, half * 4:(half + 1) * 4, :],
                               PT[b][:, 0:512].rearrange("p (k t) -> p k t", k=4), [BPT[b]], [B_oT[half]])
                        for half in range(2):
                            b = q % 4
                            q += 1
                            for kc in range(8):
                                mm(PS[b][:], oT[:, kc, :], wo[:, kc, half * 512:(half + 1) * 512], kc == 0, kc == 7,
                                   [B_oT[kc // 4], Bwo[kc]], [BPS[b]])
                            hv = h[sl][:, s, half * 512:(half + 1) * 512]
                            tt("dve", hv, hv, PS[b][:], ALU.add, [BPS[b], Bh[sl]], [Bh[sl]])
                    S.dma("sp", Hv[t], h[sl][:], r=[Bh[sl]], w=([D_H0] if t == 0 else []), chan=Bh[sl])
            S.barrier()

        for i in range(DEPTH):
            ffn_stage(i, 0, first=(i == 0))
            if i % 2 == 0:
                ret_stage1(i)
                ret_stage2(i)
            else:
                fox_stage1(i)
                with ExitStack() as fs:
                    O = fs.enter_context(nc.sbuf_tensor("O", [128, NS, D], BF16))
                    B_O = Buf()
                    fox_stage3(O, B_O)
                    fox_stage4(i, O, B_O)
            ffn_stage(i, 1, first=False)
            ple_stage(i, last=(i == DEPTH - 1))
        S.emit()
    return nc


def make_consts(T):
    H_, C = RET_H, RET_C
    half = RET_DK // 2
    inv_freq = (10000.0 ** (-np.arange(half, dtype=np.float32) / half)).astype(np.float32)
    pos = np.arange(T, dtype=np.float32)
    ang = (pos[None, :] * inv_freq[:, None]).astype(np.float32)
    cos, sin = np.cos(ang).astype(np.float32), np.sin(ang).astype(np.float32)
    sc = np.float32(RET_DK ** -0.5)
    rope = np.stack([cos, sin]).astype(np.float32)
    log_gamma = np.log1p(-np.exp2(-5.0 - np.arange(H_, dtype=np.float32))).astype(np.float32)
    idx = np.arange(C, dtype=np.float32)
    diff = idx[:, None] - idx[None, :]
    decay = np.where(diff[None] >= 0, np.exp(log_gamma[:, None, None] * np.maximum(diff, 0.0)[None]), 0.0)
    dmask = np.ascontiguousarray(decay.transpose(2, 0, 1)).astype(np.float32)
    xi = np.exp(log_gamma[:, None] * (idx + 1)[None, :]).astype(np.float32)
    xi_bc = np.ascontiguousarray(np.broadcast_to(np.repeat(xi, 2, axis=0)[None], (128, 2 * H_, C))).astype(np.float32)
    zeta = np.exp(log_gamma[:, None] * (C - 1 - idx)[None, :]).astype(np.float32).T.copy()
    zeta = np.ascontiguousarray(np.repeat(zeta, RET_DK, axis=1)).astype(np.float32)
    kk = np.arange(128)
    maskneg = np.where(kk[:, None] <= kk[None, :], 0.0, NEG).astype(np.float32)
    return {
        "c_ident": np.eye(128, dtype=np.float32),
        "c_rope": rope,
        "c_dmask": dmask,
        "c_xi": xi_bc,
        "c_zeta": zeta,
        "c_maskneg": maskneg,
    }


def make_in_maps(inputs, T, nb):
    f = lambda a: np.ascontiguousarray(np.asarray(a, dtype=np.float32))
    consts = make_consts(T)
    shared = {
        "norm_w": f(inputs["norm_w"]),
        "ffn_w_in": f(inputs["ffn_w_in"]),
        "ffn_w_out": f(inputs["ffn_w_out"]),
        "ret_w_in": f(inputs["ret_w_in"])[0],
        "ret_gn_w": f(inputs["ret_gn_w"])[0].reshape(-1),
        "ret_w_out": f(inputs["ret_w_out"])[0],
        "fox_w_in": f(inputs["fox_w_in"])[0],
        "fox_b_f": f(inputs["fox_b_f"])[0].reshape(FOX_H, 1),
        "fox_w_out": f(inputs["fox_w_out"])[0],
        "ple_w_proj": f(inputs["ple_w_proj"]),
        "ple_w_gate": f(inputs["ple_w_gate"]),
        "final_norm_w": f(inputs["final_norm_w"]),
    }
    shared.update(consts)
    x = f(inputs["x"])
    p = f(inputs["p"])
    maps = []
    for b in range(nb):
        m = dict(shared)
        m["x"] = np.ascontiguousarray(x[b])
        m["p"] = np.ascontiguousarray(p[:, b])
        maps.append(m)
    return maps


_NC_CACHE = {}


def kernel(**inputs):
    x = np.asarray(inputs["x"])
    nb, T = x.shape[0], x.shape[1]
    if T not in _NC_CACHE:
        _NC_CACHE[T] = build(T)
    nc = _NC_CACHE[T]
    in_maps = make_in_maps(inputs, T, nb)
    res = run_bass_kernel_spmd(nc, in_maps, core_ids=list(range(nb)))
    return np.stack([np.asarray(r["out"], dtype=np.float32) for r in res.results], axis=0)
```

```python
import numpy as np
from contextlib import ExitStack
import concourse.bass as bass
import concourse.mybir as mybir
from concourse.bass_utils import run_bass_kernel_spmd

F32 = mybir.dt.float32
BF16 = mybir.dt.bfloat16
AF = mybir.ActivationFunctionType
ALU = mybir.AluOpType

D = 1024
DFF = 2816
NFC = DFF // 128
DPLE = 256
DEPTH = 2
EPS = 1e-6
RET_H, RET_DK, RET_DV, RET_C = 4, 256, 512, 128
RET_IN = 6144
FOX_H, FOX_DH = 16, 64
FOX_IN = 3088
NEG = -30000.0
USE_POW = True

ENGS = ["pe", "act", "dve", "pool", "sp"]


class Buf:
    __slots__ = ("w", "rs", "sem")

    def __init__(self):
        self.w = None
        self.rs = []
        self.sem = {}


class DSem:
    __slots__ = ("sem", "count", "last")

    def __init__(self, sem):
        self.sem = sem
        self.count = 0
        self.last = None


class Op:
    __slots__ = ("eng", "fn", "deps", "sig", "cnt", "dsem", "dcnt", "epoch")

    def __init__(self, eng, fn, epoch):
        self.eng = eng
        self.fn = fn
        self.deps = []
        self.sig = False
        self.cnt = 0
        self.dsem = None
        self.dcnt = 0
        self.epoch = epoch


class Sched:
    def __init__(self, nc):
        self.nc = nc
        self.ops = {e: [] for e in ENGS}
        self.esem = {e: nc.alloc_semaphore(name="es_" + e) for e in ENGS}
        self.nsem = 0
        self.free = {e: [] for e in ENGS}
        self.live = []
        self.epoch = 0

    def _track(self, op, r, w):
        deps = []
        for b in r:
            if b.w is not None:
                deps.append(b.w)
        for b in w:
            if b.w is not None:
                deps.append(b.w)
            deps.extend(b.rs)
        for b in w:
            b.w = op
            b.rs = []
        for b in r:
            b.rs.append(op)
        seen = set()
        for d in deps:
            if d is op or id(d) in seen or d.epoch < self.epoch:
                continue
            if d.eng == "pe" and op.eng == "pe":
                continue
            seen.add(id(d))
            op.deps.append(d)
            if d.dsem is None:
                d.sig = True

    def op(self, eng, fn, r=(), w=()):
        o = Op(eng, fn, self.epoch)
        self._track(o, r, w)
        self.ops[eng].append(o)
        return o

    def dma(self, eng, out, in_, r=(), w=(), chan=None):
        if eng not in chan.sem:
            if self.free[eng]:
                ds = self.free[eng].pop()
            else:
                ds = DSem(self.nc.alloc_semaphore(name="ds_%d" % self.nsem))
                self.nsem += 1
            chan.sem[eng] = ds
            self.live.append((chan, eng, ds))
        ds = chan.sem[eng]
        ds.count += 16

        def fn(e, out=out, in_=in_):
            return e.dma_start(out=out, in_=in_)
        o = Op(eng, fn, self.epoch)
        o.dsem = ds.sem
        o.dcnt = ds.count
        ds.last = o
        self._track(o, r, w)
        self.ops[eng].append(o)
        return o

    def barrier(self):
        lasts = []
        for e in ENGS:
            for o in reversed(self.ops[e]):
                if o.dsem is None and o.fn is not None:
                    o.sig = True
                    lasts.append(o)
                    break
        dlast = [ds.last for (_, _, ds) in self.live if ds.last is not None]
        for e in ENGS:
            o = Op(e, None, self.epoch)
            o.deps = list(lasts) + dlast
            self.ops[e].append(o)
        for (b, e, ds) in self.live:
            del b.sem[e]
            self.free[e].append(ds)
        self.live = []
        self.epoch += 1

    def emit(self):
        nc = self.nc
        for e in ENGS:
            c = 0
            for o in self.ops[e]:
                if o.dsem is None and o.sig:
                    c += 1
                    o.cnt = c

        def run(ename, eng):
            waited = {}
            for o in self.ops[ename]:
                need = {}
                for d in o.deps:
                    if d.dsem is not None:
                        s, v = d.dsem, d.dcnt
                    else:
                        s, v = self.esem[d.eng], d.cnt
                    if s.num not in need or need[s.num][1] < v:
                        need[s.num] = (s, v)
                for num, (s, v) in need.items():
                    if waited.get(num, 0) >= v:
                        continue
                    waited[num] = v
                    eng.wait_ge(s, v)
                if o.fn is None:
                    continue
                ins = o.fn(eng)
                if o.dsem is not None:
                    ins.then_inc(o.dsem, 16)
                elif o.sig:
                    ins.then_inc(self.esem[ename], 1)

        with nc.Block() as block:
            @block.tensor
            def _(e):
                run("pe", e)

            @block.scalar
            def _(e):
                run("act", e)

            @block.vector
            def _(e):
                run("dve", e)

            @block.gpsimd
            def _(e):
                run("pool", e)

            @block.sync
            def _(e):
                run("sp", e)


def build(T):
    NT = T // 512
    NS = T // 128
    nc = bass.Bass("TRN2", target_bir_lowering=False)

    def din(name, shape):
        return nc.dram_tensor(name, list(shape), F32, kind="ExternalInput").ap()

    x_in = din("x", [T, D])
    p_in = din("p", [DEPTH, T, DPLE])
    norm_w = din("norm_w", [DEPTH, 4, D])
    ffn_w_in = din("ffn_w_in", [DEPTH, 2, D, 2 * DFF])
    ffn_w_out = din("ffn_w_out", [DEPTH, 2, DFF, D])
    ret_w_in = din("ret_w_in", [D, RET_IN])
    ret_gn_w = din("ret_gn_w", [RET_H * RET_DV])
    ret_w_out = din("ret_w_out", [RET_H * RET_DV, D])
    fox_w_in = din("fox_w_in", [D, FOX_IN])
    fox_b_f = din("fox_b_f", [FOX_H, 1])
    fox_w_out = din("fox_w_out", [D, D])
    ple_w_proj = din("ple_w_proj", [DEPTH, DPLE, D])
    ple_w_gate = din("ple_w_gate", [DEPTH, D, D])
    final_norm_w = din("final_norm_w", [D])
    c_ident = din("c_ident", [128, 128])
    c_rope = din("c_rope", [2, 128, T])
    c_dmask = din("c_dmask", [128, 4, 128])
    c_xi = din("c_xi", [128, 8, 128])
    c_zeta = din("c_zeta", [128, 1024])
    c_maskneg = din("c_maskneg", [128, 128])
    out = nc.dram_tensor("out", [T, D], F32, kind="ExternalOutput").ap()

    H = nc.dram_tensor("H", [T, D], F32).ap()
    QKs = nc.dram_tensor("QKs", [16 * 128, T], BF16).ap()
    VGs = nc.dram_tensor("VGs", [T, 4096], BF16).ap()
    QA = nc.dram_tensor("QA", [FOX_H, 70, T], BF16).ap()
    KA = nc.dram_tensor("KA", [FOX_H, 70, T], BF16).ap()
    Vs = nc.dram_tensor("Vs", [FOX_H, 128, NS, 64], BF16).ap()

    S = Sched(nc)
    uid = [0]

    def un(name):
        uid[0] += 1
        return "%s_%d" % (name, uid[0])
    gam = [1.0 - 2.0 ** (-5.0 - h) for h in range(RET_H)]
    gamma_c = [float(np.exp(np.log1p(-2.0 ** (-5.0 - h)) * RET_C)) for h in range(RET_H)]

    with ExitStack() as g:
        def gsb(name, shape, dt):
            return g.enter_context(nc.sbuf_tensor(un(name), list(shape), dt))

        PT = [g.enter_context(nc.psum_tensor("pt%d" % i, [128, 1024], BF16)) for i in range(2)]
        BPT = [Buf() for _ in range(2)]
        PS = [g.enter_context(nc.psum_tensor("ps%d" % i, [128, 512], F32)) for i in range(6)]
        BPS = [Buf() for _ in range(6)]
        ptc = [0]

        ident_f = gsb("ident_f", [128, 128], F32)
        ident = gsb("ident", [128, 128], BF16)
        B_identf, B_ident = Buf(), Buf()
        S.dma("sp", ident_f[:], c_ident, w=[B_identf], chan=B_identf)
        S.op("dve", lambda e: e.tensor_copy(out=ident[:], in_=ident_f[:]), r=[B_identf], w=[B_ident])

        def mm(out_, lhsT, rhs, start, stop, r, w, skip=False):
            S.op("pe", lambda e: e.matmul(out_, lhsT=lhsT, rhs=rhs, start=start, stop=stop,
                                          skip_group_check=skip), r=r, w=w)

        def tr(out_, in_, r, w):
            S.op("pe", lambda e: e.transpose(out=out_, in_=in_, identity=ident[:]), r=list(r) + [B_ident], w=w)

        def act(out_, in_, func, r, w, **kw):
            S.op("act", lambda e: e.activation(out=out_, in_=in_, func=func, **kw), r=r, w=w)

        def tt(eng, out_, in0, in1, op, r, w):
            S.op(eng, lambda e: e.tensor_tensor(out=out_, in0=in0, in1=in1, op=op), r=r, w=w)

        def ts(eng, out_, in0, s1, op0, r, w, s2=None, op1=None):
            if op1 is None:
                S.op(eng, lambda e: e.tensor_scalar(out=out_, in0=in0, scalar1=s1, scalar2=None, op0=op0), r=r, w=w)
            else:
                S.op(eng, lambda e: e.tensor_scalar(out=out_, in0=in0, scalar1=s1, scalar2=s2, op0=op0, op1=op1),
                     r=r, w=w)

        def stt(out_, in0, scalar, in1, op0, op1, r, w):
            S.op("dve", lambda e: e.scalar_tensor_tensor(out=out_, in0=in0, scalar=scalar, in1=in1, op0=op0, op1=op1),
                 r=r, w=w)

        def cp(eng, out_, in_, r, w):
            if eng == "act":
                act(out_, in_, AF.Copy, r, w)
            else:
                S.op(eng, lambda e: e.tensor_copy(out=out_, in_=in_), r=r, w=w)

        def recip(out_, in_, r, w):
            S.op("dve", lambda e: e.reciprocal(out=out_, in_=in_), r=r, w=w)

        def memset(eng, ap, val, w):
            S.op(eng, lambda e: e.memset(ap, val), w=w)

        hslot0 = gsb("hslot0", [128, 4, D], F32)
        pref = {"ready": False}
        D_H0 = Buf()
        mhalf = gsb("mhalf", [128, 4], F32)
        B_mhalf = Buf()
        memset("pool", mhalf[:], -0.5, [B_mhalf])

        def rstd_chain(ss, B_ss, tmp, B_tmp, rs, B_rs):
            ts("dve", tmp, ss, EPS, ALU.add, (B_ss if isinstance(B_ss, list) else [B_ss]), [B_tmp])
            if USE_POW:
                tt("pool", rs, tmp, mhalf[:], ALU.pow, [B_tmp, B_mhalf], [B_rs])
            else:
                act(tmp, tmp, AF.Sqrt, [B_tmp], [B_tmp])
                recip(rs, tmp, [B_tmp], [B_rs])

        Hv = H.rearrange("(t s p) d -> t p s d", s=4, p=128)
        Xv = x_in.rearrange("(t s p) d -> t p s d", s=4, p=128)
        Ov = out.rearrange("(t s p) d -> t p s d", s=4, p=128)

        class TileCtx:
            def __init__(self, es, nwsrc, src=None, nslots=2, nx=1):
                self.src = Hv if src is None else src
                sb = lambda name, shape, dt: es.enter_context(nc.sbuf_tensor(un(name), list(shape), dt))
                self.h = [hslot0] + [sb("h%d" % i, [128, 4, D], F32) for i in range(1, nslots)]
                self.Bh = [[Buf() for _ in range(4)] for _ in range(nslots)]
                self.nslots = nslots
                self.nwb = sb("nwb", [128, D], F32)
                self.B_nwb = Buf()
                S.dma("sp", self.nwb[:], nwsrc.partition_broadcast(128), w=[self.B_nwb], chan=self.B_nwb)
                self.xns = [sb("xn%d" % i, [128, D], BF16) for i in range(2)]
                self.B_xns = [Buf(), Buf()]
                self.nx = nx
                self.xnTs = [sb("xnT%d" % i, [128, 8, 512], BF16) for i in range(nx)]
                self.B_xnTs = [[Buf() for _ in range(8)] for _ in range(nx)]
                self.xnT, self.B_xnT = self.xnTs[0], self.B_xnTs[0]
                self.ss = sb("ss", [128, 4], F32)
                self.sst = sb("sst", [128, 4], F32)
                self.rs = sb("rs", [128, 4], F32)
                self.B_ss, self.B_sst, self.B_rs = [Buf() for _ in range(4)], Buf(), Buf()

            def use(self, t):
                self.xnT, self.B_xnT = self.xnTs[t % self.nx], self.B_xnTs[t % self.nx]

            def load(self, t):
                sl = t % self.nslots
                if t == 0 and pref["ready"]:
                    pref["ready"] = False
                    return
                S.dma("sp", self.h[sl][:], self.src[t], w=self.Bh[sl], chan=self.Bh[sl][0])

            def prefetch_next(self, t):
                if t == NT - 1 and (NT - 1) % self.nslots != 0:
                    S.dma("sp", hslot0[:], Hv[0], r=[D_H0], w=self.Bh[0], chan=self.Bh[0][0])
                    pref["ready"] = True

            def _stt(self, t, s):
                sl = t % self.nslots
                h, Bh = self.h[sl], self.Bh[sl]
                xn, B_xn = self.xns[s % 2], self.B_xns[s % 2]
                stt(xn[:], h[:, s, :], self.rs[:, s:s + 1], self.nwb[:], ALU.mult, ALU.mult,
                    [Bh[s], self.B_rs, self.B_nwb], [B_xn])

            def norm_stats(self, t):
                sl = t % self.nslots
                h, Bh = self.h[sl], self.Bh[sl]
                for s in range(4):
                    act(self.xns[s % 2][:], h[:, s, :], AF.Square, [Bh[s]], [self.B_xns[s % 2], self.B_ss[s]],
                        scale=1.0 / 32.0, accum_out=self.ss[:, s:s + 1])
                rstd_chain(self.ss[:], self.B_ss, self.sst[:], self.B_sst, self.rs[:], self.B_rs)
                self._stt(t, 0)

            def norm_piece(self, t, g):
                xnT, B_xnT = self.xnTs[t % self.nx], self.B_xnTs[t % self.nx]
                s, half = g // 2, g % 2
                xn, B_xn = self.xns[s % 2], self.B_xns[s % 2]
                if half == 1 and s < 3:
                    self._stt(t, s + 1)
                b = ptc[0] % 2
                ptc[0] += 1
                for k4 in range(4):
                    kc = half * 4 + k4
                    tr(PT[b][:, k4 * 128:(k4 + 1) * 128], xn[:, kc * 128:(kc + 1) * 128], [B_xn], [BPT[b]])
                eng = "act" if half == 0 else "dve"
                cp(eng, xnT[:, half * 4:(half + 1) * 4, s * 128:(s + 1) * 128],
                   PT[b][:, 0:512].rearrange("p (k t) -> p k t", k=4),
                   [BPT[b]], B_xnT[half * 4:(half + 1) * 4])

            def norm_T(self, t):
                self.norm_stats(t)
                for g in range(8):
                    self.norm_piece(t, g)

        def load_w_cols(es, name, src2d, K, N, blocks):
            kc_n = K // 128
            wt = es.enter_context(nc.sbuf_tensor(un(name), [128, kc_n, N], BF16))
            srcv = src2d.rearrange("(k p) n -> p k n", p=128)
            Bw = {}
            for (c0, c1) in blocks:
                b = Buf()
                for c in range(c0, c1, 128):
                    Bw[c] = b
                S.dma("pool", wt[:, :, c0:c1], srcv[:, :, c0:c1], w=[b], chan=b)
            return wt, (lambda col: Bw[(col // 128) * 128])

        def load_w(es, name, src2d, K, N, ncols_split=1):
            kc_n = K // 128
            wt = es.enter_context(nc.sbuf_tensor(un(name), [128, kc_n, N], BF16))
            Bw = [Buf() for _ in range(kc_n)]
            for kc in range(kc_n):
                S.dma("pool", wt[:, kc, :], src2d[kc * 128:(kc + 1) * 128, :], w=[Bw[kc]], chan=Bw[kc])
            return wt, Bw


        def ffn_stage(i, j, first):
            with ExitStack() as es:
                sb = lambda name, shape, dt: es.enter_context(nc.sbuf_tensor(un(name), list(shape), dt))
                blocks = []
                for c0 in range(0, NFC, 4):
                    c1 = min(c0 + 4, NFC)
                    blocks.append((c0 * 128, c1 * 128))
                    blocks.append((DFF + c0 * 128, DFF + c1 * 128))
                win, Bwin = load_w_cols(es, "win", ffn_w_in[i, j], D, 2 * DFF, blocks)
                wout, Bwout = load_w(es, "wout", ffn_w_out[i, j], DFF, D)
                tc = TileCtx(es, norm_w[i, 2 * j], src=Xv if first else None)
                hid = sb("hid", [128, NFC, 512], BF16)
                B_hid = [Buf() for _ in range(NFC)]
                sg = [sb("sg%d" % k, [128, 512], F32) for k in range(2)]
                B_sg = [Buf(), Buf()]
                tc.load(0)
                tc.norm_T(0)
                for t in range(NT):
                    if t + 1 < NT:
                        tc.load(t + 1)
                    tc.prefetch_next(t)
                    sl = t % 2
                    for c in range(NFC):
                        pg, pu = (0, 1) if c % 2 == 0 else (2, 3)
                        for kc in range(8):
                            mm(PS[pg][:], win[:, kc, c * 128:(c + 1) * 128], tc.xnT[:, kc, :], kc == 0, kc == 7,
                               [Bwin(c * 128), tc.B_xnT[kc]], [BPS[pg]])
                        for kc in range(8):
                            mm(PS[pu][:], win[:, kc, DFF + c * 128:DFF + (c + 1) * 128], tc.xnT[:, kc, :],
                               kc == 0, kc == 7, [Bwin(DFF + c * 128), tc.B_xnT[kc]], [BPS[pu]])
                        k = c % 2
                        act(sg[k][:], PS[pg][:], AF.Silu, [BPS[pg]], [B_sg[k]])
                        tt("dve", hid[:, c, :], sg[k][:], PS[pu][:], ALU.mult, [B_sg[k], BPS[pu]], [B_hid[c]])
                        if c == NFC // 2 and t + 1 < NT:
                            tc.norm_stats(t + 1)
                    q = 0
                    for s in range(4):
                        for half in range(2):
                            if t + 1 < NT:
                                tc.norm_piece(t + 1, q)
                            b = 4 + (q % 2)
                            q += 1
                            for c in range(NFC):
                                mm(PS[b][:], hid[:, c, s * 128:(s + 1) * 128], wout[:, c, half * 512:(half + 1) * 512],
                                   c == 0, c == NFC - 1, [B_hid[c], Bwout[c]], [BPS[b]])
                            hv = tc.h[sl][:, s, half * 512:(half + 1) * 512]
                            stt(hv, PS[b][:], 0.5, hv, ALU.mult, ALU.add, [BPS[b], tc.Bh[sl][s]], [tc.Bh[sl][s]])
                    S.dma("sp", Hv[t], tc.h[sl][:], r=tc.Bh[sl], w=([D_H0] if t == 0 else []), chan=tc.Bh[sl][0])
            S.barrier()

        def ple_stage(i, last):
            with ExitStack() as es:
                sb = lambda name, shape, dt: es.enter_context(nc.sbuf_tensor(un(name), list(shape), dt))
                wg, Bwg = load_w(es, "wg", ple_w_gate[i], D, D)
                wp, Bwp = load_w(es, "wp", ple_w_proj[i], DPLE, D)
                tc = TileCtx(es, norm_w[i, 3], nslots=3, nx=2)
                pv = p_in[i].rearrange("(t s p) d -> t p s d", s=4, p=128)
                pb = [sb("pb%d" % k, [128, 4, DPLE], BF16) for k in range(2)]
                B_pb = [Buf(), Buf()]
                pT = sb("pT", [128, 2, 512], BF16)
                B_pT = Buf()
                sgt = [sb("sgt%d" % k, [128, 512], F32) for k in range(4)]
                B_sgt = [Buf() for _ in range(4)]
                if last:
                    fnw = sb("fnw", [128, D], F32)
                    B_fnw = Buf()
                    S.dma("sp", fnw[:], final_norm_w.partition_broadcast(128), w=[B_fnw], chan=B_fnw)
                    junk = sb("junk", [128, D], BF16)
                    B_junk = Buf()
                    fss = sb("fss", [128, 4], F32)
                    fst = sb("fst", [128, 4], F32)
                    frs = sb("frs", [128, 4], F32)
                    B_fss, B_fst, B_frs = Buf(), Buf(), Buf()
                tc.load(0)
                S.dma("pool", pb[0][:], pv[0], w=[B_pb[0]], chan=B_pb[0])
                if NT > 1:
                    tc.load(1)
                tc.norm_T(0)
                if NT > 1:
                    tc.norm_stats(1)
                for t in range(NT):
                    if t + 2 < NT:
                        tc.load(t + 2)
                    if t + 1 < NT:
                        S.dma("pool", pb[(t + 1) % 2][:], pv[t + 1], w=[B_pb[(t + 1) % 2]], chan=B_pb[(t + 1) % 2])
                    if not last:
                        tc.prefetch_next(t)
                    sl = t % 3
                    psl = t % 2
                    tc.use(t)
                    for k2 in range(2):
                        b = ptc[0] % 2
                        ptc[0] += 1
                        for s in range(4):
                            tr(PT[b][:, s * 128:(s + 1) * 128], pb[psl][:, s, k2 * 128:(k2 + 1) * 128],
                               [B_pb[psl]], [BPT[b]])
                        cp("act", pT[:, k2, :], PT[b][:, 0:512], [BPT[b]], [B_pT])
                    q = 0
                    for s in range(4):
                        for half in range(2):
                            if t + 1 < NT:
                                tc.norm_piece(t + 1, q)
                            bg, bp = ((0, 1), (2, 3), (4, 5))[q % 3]
                            k = q % 4
                            q += 1
                            for kc in range(8):
                                mm(PS[bg][:], tc.xnT[:, kc, s * 128:(s + 1) * 128], wg[:, kc, half * 512:(half + 1) * 512],
                                   kc == 0, kc == 7, [tc.B_xnT[kc], Bwg[kc]], [BPS[bg]])
                            for kc in range(2):
                                mm(PS[bp][:], pT[:, kc, s * 128:(s + 1) * 128], wp[:, kc, half * 512:(half + 1) * 512],
                                   kc == 0, kc == 1, [B_pT, Bwp[kc]], [BPS[bp]])
                            act(sgt[k][:], PS[bg][:], AF.Sigmoid, [BPS[bg]], [B_sgt[k]])
                            tt("dve", sgt[k][:], sgt[k][:], PS[bp][:], ALU.mult, [B_sgt[k], BPS[bp]], [B_sgt[k]])
                            hv = tc.h[sl][:, s, half * 512:(half + 1) * 512]
                            tt("pool" if half == 0 else "dve", hv, hv, sgt[k][:], ALU.add, [B_sgt[k], tc.Bh[sl][s]],
                               [tc.Bh[sl][s]])
                    if t + 2 < NT:
                        tc.norm_stats(t + 2)
                    if last:
                        for s in range(4):
                            act(junk[:], tc.h[sl][:, s, :], AF.Square, [tc.Bh[sl][s]], [B_junk, B_fss],
                                scale=1.0 / 32.0, accum_out=fss[:, s:s + 1])
                        rstd_chain(fss[:], B_fss, fst[:], B_fst, frs[:], B_frs)
                        for s in range(4):
                            stt(tc.h[sl][:, s, :], tc.h[sl][:, s, :], frs[:, s:s + 1], fnw[:], ALU.mult, ALU.mult,
                                [tc.Bh[sl][s], B_frs, B_fnw], [tc.Bh[sl][s]])
                        S.dma("sp", Ov[t], tc.h[sl][:], r=tc.Bh[sl], chan=tc.Bh[sl][0])
                    else:
                        S.dma("sp", Hv[t], tc.h[sl][:], r=tc.Bh[sl], w=([D_H0] if t == 0 else []), chan=tc.Bh[sl][0])
            S.barrier()

        def ret_stage1(i):
            with ExitStack() as es:
                sb = lambda name, shape, dt: es.enter_context(nc.sbuf_tensor(un(name), list(shape), dt))
                win, Bwin = load_w_cols(es, "rwin", ret_w_in, D, RET_IN,
                                        [(c, c + 512) for c in range(0, RET_IN, 512)])
                tc = TileCtx(es, norm_w[i, 1], nx=2)
                rope = [sb("rope%d" % k, [128, 2, 512], F32) for k in range(2)]
                B_rope = [Buf(), Buf()]
                ropev = c_rope.rearrange("f p t -> p f t")
                qk = sb("qk", [128, 16, 512], BF16)
                B_qk = [Buf() for _ in range(16)]
                tmp = [sb("rt%d" % k, [128, 512], F32) for k in range(4)]
                B_tmp = [Buf() for _ in range(4)]
                vg = [sb("vg%d" % k, [128, 4096], BF16) for k in range(2)]
                B_vg = [Buf(), Buf()]
                QKv = QKs.rearrange("(c p) t -> p c t", p=128)
                tc.load(0)
                tc.norm_T(0)
                S.dma("sp", rope[0][:], ropev[:, :, 0:512], w=[B_rope[0]], chan=B_rope[0])
                nvg = 0
                for t in range(NT):
                    if t + 1 < NT:
                        tc.load(t + 1)
                        S.dma("sp", rope[(t + 1) % 2][:], ropev[:, :, (t + 1) * 512:(t + 2) * 512],
                              w=[B_rope[(t + 1) % 2]], chan=B_rope[(t + 1) % 2])
                    sl = t % 2
                    tc.use(t)
                    for pr in range(8):
                        if pr == 2 and t + 1 < NT:
                            tc.norm_stats(t + 1)
                        isk = pr >= 4
                        oc1, oc2 = 2 * pr, 2 * pr + 1
                        b1, b2 = (0, 1) if pr % 2 == 0 else (2, 3)
                        for (oc, b) in ((oc1, b1), (oc2, b2)):
                            for kc in range(8):
                                mm(PS[b][:], win[:, kc, oc * 128:(oc + 1) * 128], tc.xnT[:, kc, :], kc == 0, kc == 7,
                                   [Bwin(oc * 128), tc.B_xnT[kc]], [BPS[b]])
                        cos = rope[sl][:, 0, :]
                        sin = rope[sl][:, 1, :]

                        def rmul(o_, p_, tab, r_, w_, isk=isk):
                            if isk:
                                stt(o_, p_, float(RET_DK ** -0.5), tab, ALU.mult, ALU.mult, r_, w_)
                            else:
                                tt("dve", o_, p_, tab, ALU.mult, r_, w_)
                        rmul(tmp[0][:], PS[b1][:], cos, [BPS[b1], B_rope[sl]], [B_tmp[0]])
                        rmul(tmp[1][:], PS[b2][:], sin, [BPS[b2], B_rope[sl]], [B_tmp[1]])
                        tt("pool", qk[:, oc1, :], tmp[0][:], tmp[1][:], ALU.subtract, [B_tmp[0], B_tmp[1]], [B_qk[oc1]])
                        rmul(tmp[2][:], PS[b1][:], sin, [BPS[b1], B_rope[sl]], [B_tmp[2]])
                        rmul(tmp[3][:], PS[b2][:], cos, [BPS[b2], B_rope[sl]], [B_tmp[3]])
                        tt("pool", qk[:, oc2, :], tmp[2][:], tmp[3][:], ALU.add, [B_tmp[2], B_tmp[3]], [B_qk[oc2]])
                    S.dma("sp", QKv[:, :, t * 512:(t + 1) * 512], qk[:], r=B_qk, chan=B_qk[0])
                    q = 0
                    for s in range(4):
                        k = nvg % 2
                        nvg += 1
                        for n in range(8):
                            if n % 4 == 0 and t + 1 < NT:
                                tc.norm_piece(t + 1, 2 * s + n // 4)
                            b = 4 + (q % 2)
                            q += 1
                            for kc in range(8):
                                mm(PS[b][:], tc.xnT[:, kc, s * 128:(s + 1) * 128],
                                   win[:, kc, 2048 + n * 512:2048 + (n + 1) * 512], kc == 0, kc == 7,
                                   [tc.B_xnT[kc], Bwin(2048 + n * 512)], [BPS[b]])
                            act(vg[k][:, n * 512:(n + 1) * 512], PS[b][:], AF.Copy if n < 4 else AF.Silu,
                                [BPS[b]], [B_vg[k]])
                        r0 = t * 512 + s * 128
                        S.dma("sp", VGs[r0:r0 + 128, :], vg[k][:], r=[B_vg[k]], chan=B_vg[k])
            S.barrier()

        def ret_stage2(i):
            with ExitStack() as es:
                sb = lambda name, shape, dt: es.enter_context(nc.sbuf_tensor(un(name), list(shape), dt))
                wout, Bwout = load_w(es, "rwout", ret_w_out, RET_H * RET_DV, D)
                dmask = sb("dmask", [128, 4, 128], F32)
                xi = sb("xi", [128, 8, 128], F32)
                zeta = sb("zeta", [128, 1024], F32)
                gnw = sb("gnw", [128, 2048], F32)
                B_c = Buf()
                S.dma("sp", dmask[:], c_dmask, w=[B_c], chan=B_c)
                B_c2 = Buf()
                S.dma("sp", xi[:], c_xi, w=[B_c2], chan=B_c2)
                B_c3 = Buf()
                S.dma("sp", zeta[:], c_zeta, w=[B_c3], chan=B_c3)
                B_c4 = Buf()
                S.dma("sp", gnw[:], ret_gn_w.partition_broadcast(128), w=[B_c4], chan=B_c4)
                Sf = sb("Sf", [128, 8, 512], F32)
                Sb_ = sb("Sb", [128, 8, 512], BF16)
                B_Sf = [Buf() for _ in range(8)]
                B_Sb = [Buf() for _ in range(8)]
                for k in range(8):
                    memset("pool", Sf[:, k, :], 0.0, [B_Sf[k]])
                    memset("pool", Sb_[:, k, :], 0.0, [B_Sb[k]])
                NQ = 3
                qk = [sb("qkc%d" % k, [128, 16, 128], BF16) for k in range(NQ)]
                B_qk = [Buf() for _ in range(NQ)]
                vg = [sb("vgc%d" % k, [128, 4096], BF16) for k in range(NQ)]
                B_vg = [Buf() for _ in range(NQ)]
                NHC = 4
                hc = [sb("hc%d" % k, [128, D], F32) for k in range(NHC)]
                B_hc = [Buf() for _ in range(NHC)]
                sTm = [sb("sTm%d" % k, [128, 4, 128], BF16) for k in range(2)]
                B_sTm = [Buf(), Buf()]
                qx = [sb("qx%d" % k, [128, 8, 128], BF16) for k in range(2)]
                B_qx = [Buf(), Buf()]
                kz = [sb("kz%d" % k, [128, 1024], BF16) for k in range(2)]
                B_kz = [Buf(), Buf()]
                ot = [sb("ot%d" % k, [128, 512], F32) for k in range(2)]
                B_ot = [Buf(), Buf()]
                ys = [sb("y%d" % k, [128, 2048], BF16) for k in range(2)]
                B_ys = [[Buf() for _ in range(4)] for _ in range(2)]
                yT = sb("yT", [128, 16, 128], BF16)
                B_yT = [Buf() for _ in range(4)]
                junk = sb("rjunk", [128, 512], BF16)
                B_junk = Buf()
                gss = sb("gss", [128, 4], F32)
                gst = sb("gst", [128, 4], F32)
                grs = sb("grs", [128, 4], F32)
                B_gss, B_gst, B_grs = Buf(), Buf(), Buf()
                QKv = QKs.rearrange("(c p) t -> p c t", p=128)
                Hc = H.rearrange("(n p) d -> n p d", p=128)

                def load(c):
                    k = c % NQ
                    S.dma("sp", qk[k][:], QKv[:, :, c * 128:(c + 1) * 128], w=[B_qk[k]], chan=B_qk[k])
                    S.dma("sp", vg[k][:], VGs[c * 128:(c + 1) * 128, :], w=[B_vg[k]], chan=B_vg[k])
                    S.dma("sp", hc[c % NHC][:], Hc[c], w=[B_hc[c % NHC]], chan=B_hc[c % NHC])

                def phaseA(c):
                    k = c % 2
                    kq = c % NQ
                    for h in range(4):
                        qA, qB = qk[kq][:, 2 * h, :], qk[kq][:, 2 * h + 1, :]
                        kA, kB = qk[kq][:, 8 + 2 * h, :], qk[kq][:, 8 + 2 * h + 1, :]
                        mm(PS[4][:, h * 128:(h + 1) * 128], kA, qA, True, False, [B_qk[kq]], [BPS[4]])
                        mm(PS[4][:, h * 128:(h + 1) * 128], kB, qB, False, True, [B_qk[kq]], [BPS[4]])
                    b = ptc[0] % 2
                    ptc[0] += 1
                    for kc in range(8):
                        tr(PT[b][:, kc * 128:(kc + 1) * 128], qk[kq][:, 8 + kc, :], [B_qk[kq]], [BPT[b]])
                    tt("dve", sTm[k][:], PS[4][:].rearrange("p (h t) -> p h t", h=4), dmask[:], ALU.mult,
                       [BPS[4], B_c], [B_sTm[k]])
                    tt("dve", kz[k][:], PT[b][:], zeta[:], ALU.mult, [BPT[b], B_c3], [B_kz[k]])
                    tt("pool", qx[k][:], qk[kq][:, 0:8, :], xi[:], ALU.mult, [B_qk[kq], B_c2], [B_qx[k]])

                def phaseC(c):
                    k = c % 2
                    kq = c % NQ
                    n = 0
                    for h in range(4):
                        vh = vg[kq][:, h * 512:(h + 1) * 512]
                        mm(PS[h][:], sTm[k][:, h, :], vh, True, False, [B_sTm[k], B_vg[kq]], [BPS[h]])
                        mm(PS[h][:], qx[k][:, 2 * h, :], Sb_[:, 2 * h, :], False, False, [B_qx[k], B_Sb[2 * h]], [BPS[h]])
                        mm(PS[h][:], qx[k][:, 2 * h + 1, :], Sb_[:, 2 * h + 1, :], False, True,
                           [B_qx[k], B_Sb[2 * h + 1]], [BPS[h]])
                        for dc in range(2):
                            b = 4 + (n % 2)
                            n += 1
                            sidx = 2 * h + dc
                            mm(PS[b][:], kz[k][:, sidx * 128:(sidx + 1) * 128], vh, True, True,
                               [B_kz[k], B_vg[kq]], [BPS[b]])
                            stt(Sf[:, sidx, :], Sf[:, sidx, :], gamma_c[h], PS[b][:], ALU.mult, ALU.add,
                                [BPS[b], B_Sf[sidx]], [B_Sf[sidx]])
                            cp("act" if dc == 0 else "pool", Sb_[:, sidx, :], Sf[:, sidx, :], [B_Sf[sidx]], [B_Sb[sidx]])

                def phaseD(c):
                    k = c % 2
                    kq = c % NQ
                    y, B_y = ys[c % 2], B_ys[c % 2]
                    for h in range(4):
                        act(junk[:], PS[h][:], AF.Square, [BPS[h]], [B_junk, B_gss],
                            scale=float(1.0 / np.sqrt(512.0)), accum_out=gss[:, h:h + 1])
                    rstd_chain(gss[:], B_gss, gst[:], B_gst, grs[:], B_grs)
                    for h in range(4):
                        o2 = h % 2
                        stt(ot[o2][:], PS[h][:], grs[:, h:h + 1], gnw[:, h * 512:(h + 1) * 512], ALU.mult, ALU.mult,
                            [BPS[h], B_grs, B_c4], [B_ot[o2]])
                        tt("pool", y[:, h * 512:(h + 1) * 512], ot[o2][:],
                           vg[kq][:, 2048 + h * 512:2048 + (h + 1) * 512],
                           ALU.mult, [B_ot[o2], B_vg[kq]], [B_y[h]])

                def phaseB(c):
                    k3 = c % NHC
                    y, B_y = ys[c % 2], B_ys[c % 2]
                    for gq in range(4):
                        b = ptc[0] % 2
                        ptc[0] += 1
                        for k4 in range(4):
                            kc = gq * 4 + k4
                            tr(PT[b][:, k4 * 128:(k4 + 1) * 128], y[:, kc * 128:(kc + 1) * 128], [B_y[gq]], [BPT[b]])
                        cp("act", yT[:, gq * 4:(gq + 1) * 4, :], PT[b][:, 0:512].rearrange("p (k t) -> p k t", k=4),
                           [BPT[b]], [B_yT[gq]])
                    for half in range(2):
                        b = 4 + half
                        for kc in range(16):
                            mm(PS[b][:], yT[:, kc, :], wout[:, kc, half * 512:(half + 1) * 512], kc == 0, kc == 15,
                               [B_yT[kc // 4], Bwout[kc]], [BPS[b]])
                        hv = hc[k3][:, half * 512:(half + 1) * 512]
                        tt("dve", hv, hv, PS[b][:], ALU.add, [BPS[b], B_hc[k3]], [B_hc[k3]])
                    S.dma("sp", Hc[c], hc[k3][:], r=[B_hc[k3]], w=([D_H0] if c < 4 else []), chan=B_hc[k3])

                load(0)
                if NS > 1:
                    load(1)
                phaseA(0)
                for c in range(NS):
                    phaseC(c)
                    if c + 2 < NS:
                        load(c + 2)
                    if c + 1 < NS:
                        phaseA(c + 1)
                    phaseD(c)
                    if c > 0:
                        phaseB(c - 1)
                    if c == NS - 1 and NS > 5:
                        B_p0 = Buf()
                        S.dma("sp", hslot0[:], Hv[0], r=[D_H0], w=[B_p0], chan=B_p0)
                        pref["ready"] = True
                phaseB(NS - 1)
            S.barrier()

        def fox_stage1(i):
            with ExitStack() as es:
                sb = lambda name, shape, dt: es.enter_context(nc.sbuf_tensor(un(name), list(shape), dt))
                win, Bwin = load_w_cols(es, "fwin", fox_w_in, D, FOX_IN,
                                        [(c, c + 512) for c in range(0, 3072, 512)] + [(3072, 3088)])
                tc = TileCtx(es, norm_w[i, 1], nx=2)
                negb = sb("negb", [16, 1], F32)
                B_negb = Buf()
                S.dma("sp", negb[:], fox_b_f, w=[B_negb], chan=B_negb)
                ts("dve", negb[:], negb[:], -1.0, ALU.mult, [B_negb], [B_negb])
                qs = [sb("qs%d" % k, [128, 8, 512], BF16) for k in range(2)]
                B_qs = [Buf(), Buf()]
                vs = [sb("vs%d" % k, [128, 1024], BF16) for k in range(2)]
                B_vs = [Buf(), Buf()]
                ef = sb("ef", [16, 512], F32)
                B_ef = Buf()
                lf = [sb("lf%d" % k, [16, 512], F32) for k in range(2)]
                B_lf = [Buf(), Buf()]
                ones = sb("ones", [16, 512], F32)
                CS = sb("CS", [16, T], F32)
                r1 = sb("r1", [16, 512], F32)
                P3 = sb("P3", [16, 3, 512], BF16)
                QC = [sb("QC%d" % k, [16, 3, 512], BF16) for k in range(2)]
                KC = [sb("KC%d" % k, [16, 3, 512], BF16) for k in range(2)]
                C1 = sb("C1", [16, 3, 512], BF16)
                C8 = sb("C8", [16, 3, 512], BF16)
                B_ones, B_CS, B_r1, B_P3, B_C1, B_C8 = Buf(), Buf(), Buf(), Buf(), Buf(), Buf()
                B_QC, B_KC = [Buf(), Buf()], [Buf(), Buf()]
                memset("pool", ones[:], 1.0, [B_ones])
                memset("pool", C1[:], 0.125, [B_C1])
                memset("pool", C8[:], 8.0, [B_C8])
                QAv = QA.rearrange("(m two) r t -> two r m t", two=2)
                KAv = KA.rearrange("(m two) r t -> two r m t", two=2)
                Vv = Vs.rearrange("h p n d -> p n h d")
                tc.load(0)
                tc.norm_T(0)
                nv = 0
                pending = []
                for t in range(NT):
                    if t + 1 < NT:
                        tc.load(t + 1)
                    tc.use(t)
                    q = 0
                    for which in range(2):
                        for m in range(8):
                            if which == 0 and m == 2 and t + 1 < NT:
                                tc.norm_stats(t + 1)
                            b = q % 4
                            q += 1
                            oc = which * 8 + m
                            for kc in range(8):
                                mm(PS[b][:], win[:, kc, oc * 128:(oc + 1) * 128], tc.xnT[:, kc, :], kc == 0, kc == 7,
                                   [Bwin(oc * 128), tc.B_xnT[kc]], [BPS[b]])
                            if m % 2 == 0:
                                act(qs[which][:, m, :], PS[b][:], AF.Copy, [BPS[b]], [B_qs[which]],
                                    scale=(0.125 if which == 0 else 1.0))
                            else:
                                ts("dve", qs[which][:, m, :], PS[b][:], (0.125 if which == 0 else 1.0), ALU.mult,
                                   [BPS[b]], [B_qs[which]])
                            if pending:
                                pending.pop(0)()
                        dst = QAv if which == 0 else KAv
                        for two in range(2):
                            S.dma("sp", dst[two][0:64, :, t * 512:(t + 1) * 512], qs[which][two * 64:(two + 1) * 64, :, :],
                                  r=[B_qs[which]], chan=B_qs[which])
                    for s in range(4):
                        k = nv % 2
                        nv += 1
                        for half in range(2):
                            if t + 1 < NT:
                                tc.norm_piece(t + 1, 2 * s + half)
                            b = 4 + half
                            for kc in range(8):
                                mm(PS[b][:], tc.xnT[:, kc, s * 128:(s + 1) * 128],
                                   win[:, kc, 2048 + half * 512:2048 + (half + 1) * 512], kc == 0, kc == 7,
                                   [tc.B_xnT[kc], Bwin(2048 + half * 512)], [BPS[b]])
                            cp("act" if half == 0 else "dve", vs[k][:, half * 512:(half + 1) * 512], PS[b][:],
                               [BPS[b]], [B_vs[k]])
                        S.dma("sp", Vv[:, t * 4 + s, :, :], vs[k][:].rearrange("p (h d) -> p h d", h=16),
                              r=[B_vs[k]], chan=B_vs[k])
                    for kc in range(8):
                        mm(PS[0][0:16, :], win[:, kc, 3072:3088], tc.xnT[:, kc, :], kc == 0, kc == 7,
                           [Bwin(3072), tc.B_xnT[kc]], [BPS[0]])
                    act(ef[:], PS[0][0:16, :], AF.Exp, [BPS[0], B_negb], [B_ef], scale=-1.0, bias=negb[:])
                    k2 = t % 2
                    act(lf[k2][:], ef[:], AF.Ln, [B_ef], [B_lf[k2]], bias=1.0)
                    cst = CS[:, t * 512:(t + 1) * 512]
                    init = 0.0 if t == 0 else CS[:, t * 512 - 1:t * 512]
                    S.op("dve", lambda e, cst=cst, init=init, k2=k2: e.tensor_tensor_scan(
                        out=cst, data0=ones[:], data1=lf[k2][:], initial=init, op0=ALU.mult, op1=ALU.add),
                        r=[B_ones, B_lf[k2], B_CS], w=[B_CS])
                    tsl = slice(t * 512, (t + 1) * 512)

                    def chain(cst=cst, k2=k2, tsl=tsl):
                        return [
                            lambda: cp("dve", P3[:, 0, :], cst, [B_CS], [B_P3]),
                            lambda: tt("dve", r1[:], cst, P3[:, 0, :], ALU.subtract, [B_CS, B_P3], [B_r1]),
                            lambda: cp("dve", P3[:, 1, :], r1[:], [B_r1], [B_P3]),
                            lambda: tt("dve", r1[:], r1[:], P3[:, 1, :], ALU.subtract, [B_r1, B_P3], [B_r1]),
                            lambda: cp("dve", P3[:, 2, :], r1[:], [B_r1], [B_P3]),
                            lambda: ts("dve", QC[k2][:], P3[:], -0.125, ALU.mult, [B_P3], [B_QC[k2]]),
                            lambda: ts("dve", KC[k2][:], P3[:], 8.0, ALU.mult, [B_P3], [B_KC[k2]]),
                            lambda: (S.dma("sp", QA[:, 64:67, tsl], QC[k2][:], r=[B_QC[k2]], chan=B_QC[k2]),
                                     S.dma("sp", KA[:, 67:70, tsl], KC[k2][:], r=[B_KC[k2]], chan=B_KC[k2]),
                                     S.dma("sp", QA[:, 67:70, tsl], C1[:], r=[B_C1], chan=B_C1),
                                     S.dma("sp", KA[:, 64:67, tsl], C8[:], r=[B_C8], chan=B_C8)),
                        ]
                    pending.extend(chain())
                while pending:
                    pending.pop(0)()
            S.barrier()

        def fox_stage3(O, B_O):
            with ExitStack() as es:
                sb = lambda name, shape, dt: es.enter_context(nc.sbuf_tensor(un(name), list(shape), dt))
                mneg_f = sb("mneg_f", [128, 128], F32)
                mneg = sb("mneg", [128, 128], BF16)
                B_mf, B_m = Buf(), Buf()
                S.dma("sp", mneg_f[:], c_maskneg, w=[B_mf], chan=B_mf)
                cp("dve", mneg[:], mneg_f[:], [B_mf], [B_m])
                qa = [sb("qa%d" % k, [128, T], BF16) for k in range(2)]
                ka = [sb("ka%d" % k, [128, T], BF16) for k in range(2)]
                va = [sb("va%d" % k, [128, NS, 65], BF16) for k in range(2)]
                B_qa, B_ka, B_va = [Buf(), Buf()], [Buf(), Buf()], [Buf(), Buf()]
                for k in range(2):
                    memset("pool", va[k][:, :, 64:65], 1.0, [B_va[k]])
                NPT = 4
                pt = [sb("ptx%d" % k, [128, 512], BF16) for k in range(NPT)]
                B_pt = [Buf() for _ in range(NPT)]
                rl = sb("rl", [128, 4], F32)
                B_rl = Buf()

                def load(h):
                    k = h % 2
                    S.dma("sp", qa[k][0:70, :], QA[h], w=[B_qa[k]], chan=B_qa[k])
                    S.dma("sp", ka[k][0:70, :], KA[h], w=[B_ka[k]], chan=B_ka[k])
                    S.dma("sp", va[k][:, :, 0:64], Vs[h], w=[B_va[k]], chan=B_va[k])
                load(0)
                B_p0 = Buf()
                S.dma("sp", hslot0[:], Hv[0], w=[B_p0], chan=B_p0)
                pref["ready"] = True
                items = [(h, iq, j) for h in range(FOX_H) for iq in range(NT) for j in range(4 * iq + 4)]
                LAG = 3

                def score(idx):
                    h, iq, j = items[idx]
                    k = h % 2
                    sb_ = idx % NPT
                    jj = j - 4 * iq
                    c0 = max(jj, 0) * 128
                    mm(PS[sb_][:, c0:512], ka[k][0:70, j * 128:(j + 1) * 128],
                       qa[k][0:70, iq * 512 + c0:(iq + 1) * 512], True, True,
                       [B_ka[k], B_qa[k]], [BPS[sb_]])
                    if jj >= 0:
                        mm(PS[sb_][:, c0:c0 + 128], ident[:], mneg[:], False, True, [B_ident, B_m], [BPS[sb_]],
                           skip=True)
                    act(pt[sb_][:, c0:512], PS[sb_][:, c0:512], AF.Exp, [BPS[sb_]], [B_pt[sb_]])

                def pv(idx):
                    h, iq, j = items[idx]
                    k = h % 2
                    sb_ = idx % NPT
                    jj = j - 4 * iq
                    ob = 4 + ((h * NT + iq) % 2)
                    if iq == 0 and j == 0 and h + 1 < FOX_H:
                        load(h + 1)
                    for qs_ in range(max(jj, 0), 4):
                        last = (j == 4 * iq + qs_)
                        mm(PS[ob][:, qs_ * 65:(qs_ + 1) * 65], pt[sb_][:, qs_ * 128:(qs_ + 1) * 128],
                           va[k][:, j, :], (j == 0 and qs_ == 0), last, [B_pt[sb_], B_va[k]], [BPS[ob]],
                           skip=True)
                    if j == 4 * iq + 3:
                        ov = PS[ob][:, 0:260].rearrange("p (a b) -> p a b", b=65)
                        recip(rl[:], ov[:, :, 64], [BPS[ob]], [B_rl])
                        for qs_ in range(4):
                            ts("dve", O[:, iq * 4 + qs_, h * 64:(h + 1) * 64], PS[ob][:, qs_ * 65:qs_ * 65 + 64],
                               rl[:, qs_:qs_ + 1], ALU.mult, [BPS[ob], B_rl], [B_O])

                for idx in range(len(items) + LAG):
                    if idx < len(items):
                        score(idx)
                    if idx >= LAG:
                        pv(idx - LAG)
            S.barrier()

        def fox_stage4(i, O, B_O):
            with ExitStack() as es:
                sb = lambda name, shape, dt: es.enter_context(nc.sbuf_tensor(un(name), list(shape), dt))
                wo, Bwo = load_w(es, "fwo", fox_w_out, D, D)
                h = [hslot0, sb("fh1", [128, 4, D], F32)]
                Bh = [Buf(), Buf()]
                oT = sb("oT", [128, 8, 128], BF16)
                B_oT = [Buf(), Buf()]
                if pref["ready"]:
                    pref["ready"] = False
                else:
                    S.dma("sp", h[0][:], Hv[0], w=[Bh[0]], chan=Bh[0])
                q = 0
                for t in range(NT):
                    if t + 1 < NT:
                        S.dma("sp", h[(t + 1) % 2][:], Hv[t + 1], w=[Bh[(t + 1) % 2]], chan=Bh[(t + 1) % 2])
                    sl = t % 2
                    if t == NT - 1 and (NT - 1) % 2 != 0:
                        S.dma("sp", hslot0[:], Hv[0], r=[D_H0], w=[Bh[0]], chan=Bh[0])
                        pref["ready"] = True
                    for s in range(4):
                        n = t * 4 + s
                        for half in range(2):
                            b = ptc[0] % 2
                            ptc[0] += 1
                            for k4 in range(4):
                                kc = half * 4 + k4
                                tr(PT[b][:, k4 * 128:(k4 + 1) * 128], O[:, n, kc * 128:(kc + 1) * 128], [B_O], [BPT[b]])
                            cp("act", oT[:, half * 4:(half + 1) * 4, :],
                               PT[b][:, 0:512].rearrange("p (k t) -> p k t", k=4), [BPT[b]], [B_oT[half]])
                        for half in range(2):
                            b = q % 4
                            q += 1
                            for kc in range(8):
                                mm(PS[b][:], oT[:, kc, :], wo[:, kc, half * 512:(half + 1) * 512], kc == 0, kc == 7,
                                   [B_oT[kc // 4], Bwo[kc]], [BPS[b]])
                            hv = h[sl][:, s, half * 512:(half + 1) * 512]
                            tt("dve", hv, hv, PS[b][:], ALU.add, [BPS[b], Bh[sl]], [Bh[sl]])
                    S.dma("sp", Hv[t], h[sl][:], r=[Bh[sl]], w=([D_H0] if t == 0 else []), chan=Bh[sl])
            S.barrier()

        for i in range(DEPTH):
            ffn_stage(i, 0, first=(i == 0))
            if i % 2 == 0:
                ret_stage1(i)
                ret_stage2(i)
            else:
                fox_stage1(i)
                with ExitStack() as fs:
                    O = fs.enter_context(nc.sbuf_tensor("O", [128, NS, D], BF16))
                    B_O = Buf()
                    fox_stage3(O, B_O)
                    fox_stage4(i, O, B_O)
            ffn_stage(i, 1, first=False)
            ple_stage(i, last=(i == DEPTH - 1))
        S.emit()
    return nc


def make_consts(T):
    H_, C = RET_H, RET_C
    half = RET_DK // 2
    inv_freq = (10000.0 ** (-np.arange(half, dtype=np.float32) / half)).astype(np.float32)
    pos = np.arange(T, dtype=np.float32)
    ang = (pos[None, :] * inv_freq[:, None]).astype(np.float32)
    cos, sin = np.cos(ang).astype(np.float32), np.sin(ang).astype(np.float32)
    sc = np.float32(RET_DK ** -0.5)
    rope = np.stack([cos, sin]).astype(np.float32)
    log_gamma = np.log1p(-np.exp2(-5.0 - np.arange(H_, dtype=np.float32))).astype(np.float32)
    idx = np.arange(C, dtype=np.float32)
    diff = idx[:, None] - idx[None, :]
    decay = np.where(diff[None] >= 0, np.exp(log_gamma[:, None, None] * np.maximum(diff, 0.0)[None]), 0.0)
    dmask = np.ascontiguousarray(decay.transpose(2, 0, 1)).astype(np.float32)
    xi = np.exp(log_gamma[:, None] * (idx + 1)[None, :]).astype(np.float32)
    xi_bc = np.ascontiguousarray(np.broadcast_to(np.repeat(xi, 2, axis=0)[None], (128, 2 * H_, C))).astype(np.float32)
    zeta = np.exp(log_gamma[:, None] * (C - 1 - idx)[None, :]).astype(np.float32).T.copy()
    zeta = np.ascontiguousarray(np.repeat(zeta, RET_DK, axis=1)).astype(np.float32)
    kk = np.arange(128)
    maskneg = np.where(kk[:, None] <= kk[None, :], 0.0, NEG).astype(np.float32)
    return {
        "c_ident": np.eye(128, dtype=np.float32),
        "c_rope": rope,
        "c_dmask": dmask,
        "c_xi": xi_bc,
        "c_zeta": zeta,
        "c_maskneg": maskneg,
    }


def make_in_maps(inputs, T, nb):
    f = lambda a: np.ascontiguousarray(np.asarray(a, dtype=np.float32))
    consts = make_consts(T)
    shared = {
        "norm_w": f(inputs["norm_w"]),
        "ffn_w_in": f(inputs["ffn_w_in"]),
        "ffn_w_out": f(inputs["ffn_w_out"]),
        "ret_w_in": f(inputs["ret_w_in"])[0],
        "ret_gn_w": f(inputs["ret_gn_w"])[0].reshape(-1),
        "ret_w_out": f(inputs["ret_w_out"])[0],
        "fox_w_in": f(inputs["fox_w_in"])[0],
        "fox_b_f": f(inputs["fox_b_f"])[0].reshape(FOX_H, 1),
        "fox_w_out": f(inputs["fox_w_out"])[0],
        "ple_w_proj": f(inputs["ple_w_proj"]),
        "ple_w_gate": f(inputs["ple_w_gate"]),
        "final_norm_w": f(inputs["final_norm_w"]),
    }
    shared.update(consts)
    x = f(inputs["x"])
    p = f(inputs["p"])
    maps = []
    for b in range(nb):
        m = dict(shared)
        m["x"] = np.ascontiguousarray(x[b])
        m["p"] = np.ascontiguousarray(p[:, b])
        maps.append(m)
    return maps


_NC_CACHE = {}


def kernel(**inputs):
    x = np.asarray(inputs["x"])
    nb, T = x.shape[0], x.shape[1]
    if T not in _NC_CACHE:
        _NC_CACHE[T] = build(T)
    nc = _NC_CACHE[T]
    in_maps = make_in_maps(inputs, T, nb)
    res = run_bass_kernel_spmd(nc, in_maps, core_ids=list(range(nb)))
    return np.stack([np.asarray(r["out"], dtype=np.float32) for r in res.results], axis=0)
```
